# Optimizing a Trainium2 kernel written in Bass

```python
import math
import jax, jax.numpy as jnp
from jax import lax
import numpy as np

D_MODEL = 1024
BATCH = 16
SEQ = 2048
DEPTH = 1

CHUNK = 64
Q_BLOCK = 128
N_MEM = 256
EPS = 1e-6

DA_HEADS = 4
DA_QK_DIM = 64
DA_V_DIM = 2 * DA_QK_DIM
DA_WIDTH = DA_HEADS * DA_V_DIM
FX_HEADS = 8
FX_DIM = 64
FX_WIDTH = FX_HEADS * FX_DIM
MEM_HEADS = 4
MEM_DIM = 128
MEM_WIDTH = MEM_HEADS * MEM_DIM
N_BRANCH = 3
D_FF = 2816
CONV_W = 3

IN_SIZES = (DA_HEADS * 2 * DA_QK_DIM, DA_HEADS * 2 * DA_QK_DIM, DA_WIDTH,
            FX_WIDTH, FX_WIDTH, FX_WIDTH, FX_HEADS,
            MEM_WIDTH,
            N_BRANCH * D_MODEL)
IN_COLS = sum(IN_SIZES)

kernel_name = "hybrid_diffattn_fox_memxattn_convffn"


def rms_norm(x, g):
    xf = x.astype(jnp.float32)
    y = xf * lax.rsqrt(jnp.mean(xf * xf, axis=-1, keepdims=True) + EPS)
    return (y * g.astype(jnp.float32)).astype(x.dtype)


def lambda_init(layer_idx):
    return 0.8 - 0.6 * math.exp(-0.3 * layer_idx)


def alibi_slopes(n):
    return jnp.array([2.0 ** (-8.0 * (i + 1) / n) for i in range(n)], dtype=jnp.float32)


def split_heads(t, n_heads, d):
    b, s, _ = t.shape
    return t.reshape(b, s, n_heads, d).transpose(0, 2, 1, 3)


def merge_heads(t):
    b, h, s, d = t.shape
    return t.transpose(0, 2, 1, 3).reshape(b, s, h * d)


def diff_attention(q, k, v, lam, slopes):
    s_len = q.shape[3]
    scale = DA_QK_DIM ** -0.5
    pos = jnp.arange(s_len)
    outs = []
    for i in range(s_len // Q_BLOCK):
        q0, q1 = i * Q_BLOCK, (i + 1) * Q_BLOCK
        tq, tk = pos[q0:q1], pos[:q1]
        s = jnp.einsum('bhcqd,bhckd->bhcqk', q[:, :, :, q0:q1], k[:, :, :, :q1]).astype(jnp.float32) * scale
        dist = jnp.abs(tq[:, None] - tk[None, :]).astype(jnp.float32)
        bias = -slopes[:, None, None] * dist
        mask = (tk[None, :] // CHUNK) <= (tq[:, None] // CHUNK)
        s = jnp.where(mask, s + bias[None, :, None], -jnp.inf)
        p = jax.nn.softmax(s, axis=-1)
        w = p[:, :, 0] - lam * p[:, :, 1]
        outs.append(jnp.einsum('bhqk,bhkd->bhqd', w.astype(v.dtype), v[:, :, :q1]))
    return jnp.concatenate(outs, axis=2)


def forgetting_attention(q, k, v, log_f):
    s_len = q.shape[2]
    scale = FX_DIM ** -0.5
    c = jnp.cumsum(log_f, axis=-1)
    pos = jnp.arange(s_len)
    outs = []
    for i in range(s_len // Q_BLOCK):
        q0, q1 = i * Q_BLOCK, (i + 1) * Q_BLOCK
        tq, tk = pos[q0:q1], pos[:q1]
        s = jnp.einsum('bhqd,bhkd->bhqk', q[:, :, q0:q1], k[:, :, :q1]).astype(jnp.float32) * scale
        s = s + c[:, :, q0:q1, None] - c[:, :, None, :q1]
        s = jnp.where(tk[None, :] <= tq[:, None], s, -jnp.inf)
        p = jax.nn.softmax(s, axis=-1)
        outs.append(jnp.einsum('bhqk,bhkd->bhqd', p.astype(v.dtype), v[:, :, :q1]))
    return jnp.concatenate(outs, axis=2)


def memory_attention(q, k, v):
    s = jnp.einsum('bhqd,bhkd->bhqk', q, k).astype(jnp.float32) * (MEM_DIM ** -0.5)
    p = jax.nn.softmax(s, axis=-1)
    return jnp.einsum('bhqk,bhkd->bhqd', p.astype(v.dtype), v)


def setup_inputs(seed: int = 0) -> dict:
    key = jax.random.key(seed)
    ks = jax.random.split(key, 32)
    f32 = jnp.float32

    def nrm(k, shape, scale):
        return jax.random.normal(k, shape, f32) * scale

    def gain(k, shape):
        return 1.0 + 0.05 * jax.random.normal(k, shape, f32)

    L = DEPTH
    return {
        "x": nrm(ks[0], (BATCH, SEQ, D_MODEL), 1.0),
        "mem": nrm(ks[1], (BATCH, N_MEM, D_MODEL), 1.0),
        "norm_mix": gain(ks[2], (L, D_MODEL)),
        "w_in": nrm(ks[3], (L, D_MODEL, IN_COLS), D_MODEL ** -0.5),
        "b_gate": nrm(ks[4], (L, N_BRANCH, D_MODEL), 0.1),
        "da_q_norm": gain(ks[5], (L, DA_QK_DIM)),
        "da_k_norm": gain(ks[6], (L, DA_QK_DIM)),
        "da_lambda_q1": nrm(ks[7], (L, DA_QK_DIM), 0.1),
        "da_lambda_k1": nrm(ks[8], (L, DA_QK_DIM), 0.1),
        "da_lambda_q2": nrm(ks[9], (L, DA_QK_DIM), 0.1),
        "da_lambda_k2": nrm(ks[10], (L, DA_QK_DIM), 0.1),
        "da_subln": gain(ks[11], (L, DA_V_DIM)),
        "fx_q_norm": gain(ks[12], (L, FX_DIM)),
        "fx_k_norm": gain(ks[13], (L, FX_DIM)),
        "fx_f_bias": 3.0 + 0.5 * jax.random.normal(ks[14], (L, FX_HEADS), f32),
        "mem_norm": gain(ks[15], (L, D_MODEL)),
        "w_mem_kv": nrm(ks[16], (L, D_MODEL, 2 * MEM_WIDTH), D_MODEL ** -0.5),
        "mem_q_norm": gain(ks[17], (L, MEM_DIM)),
        "mem_k_norm": gain(ks[18], (L, MEM_DIM)),
        "w_branch_da": nrm(ks[19], (L, DA_WIDTH, D_MODEL), DA_WIDTH ** -0.5),
        "w_branch_fx": nrm(ks[20], (L, FX_WIDTH, D_MODEL), FX_WIDTH ** -0.5),
        "w_branch_mem": nrm(ks[21], (L, MEM_WIDTH, D_MODEL), MEM_WIDTH ** -0.5),
        "w_out": nrm(ks[22], (L, D_MODEL, D_MODEL), D_MODEL ** -0.5),
        "norm_ffn": gain(ks[23], (L, D_MODEL)),
        "w_up": nrm(ks[24], (L, D_MODEL, 2 * D_FF), D_MODEL ** -0.5),
        "conv_w": nrm(ks[25], (L, CONV_W, 2 * D_FF), CONV_W ** -0.5),
        "conv_b": nrm(ks[26], (L, 2 * D_FF), 0.02),
        "w_down": nrm(ks[27], (L, D_FF, D_MODEL), D_FF ** -0.5),
    }


def reference(x, mem, norm_mix, w_in, b_gate, da_q_norm, da_k_norm,
              da_lambda_q1, da_lambda_k1, da_lambda_q2, da_lambda_k2, da_subln,
              fx_q_norm, fx_k_norm, fx_f_bias, mem_norm, w_mem_kv, mem_q_norm, mem_k_norm,
              w_branch_da, w_branch_fx, w_branch_mem, w_out,
              norm_ffn, w_up, conv_w, conv_b, w_down):
    b, s_len, _ = x.shape
    offsets = np.cumsum(np.array(IN_SIZES))[:-1].tolist()
    slopes = alibi_slopes(DA_HEADS)
    for l in range(DEPTH):
        h = rms_norm(x, norm_mix[l])
        proj = h @ w_in[l]
        (a_q, a_k, a_v, f_q, f_k, f_v, f_gate, m_q, g_logit) = jnp.split(proj, offsets, axis=-1)

        lam_init = lambda_init(l)
        qa = a_q.reshape(b, s_len, DA_HEADS, 2, DA_QK_DIM).transpose(0, 2, 3, 1, 4)
        ka = a_k.reshape(b, s_len, DA_HEADS, 2, DA_QK_DIM).transpose(0, 2, 3, 1, 4)
        qa = rms_norm(qa, da_q_norm[l])
        ka = rms_norm(ka, da_k_norm[l])
        va = split_heads(a_v, DA_HEADS, DA_V_DIM)
        lam = (jnp.exp(jnp.sum(da_lambda_q1[l].astype(jnp.float32) * da_lambda_k1[l].astype(jnp.float32)))
               - jnp.exp(jnp.sum(da_lambda_q2[l].astype(jnp.float32) * da_lambda_k2[l].astype(jnp.float32)))
               + lam_init)
        oa = diff_attention(qa, ka, va, lam, slopes)
        oa = rms_norm(oa, da_subln[l]) * (1.0 - lam_init)
        ya = merge_heads(oa) @ w_branch_da[l]

        qf = rms_norm(split_heads(f_q, FX_HEADS, FX_DIM), fx_q_norm[l])
        kf = rms_norm(split_heads(f_k, FX_HEADS, FX_DIM), fx_k_norm[l])
        vf = split_heads(f_v, FX_HEADS, FX_DIM)
        log_f = jax.nn.log_sigmoid((f_gate + fx_f_bias[l]).astype(jnp.float32)).transpose(0, 2, 1)
        of = forgetting_attention(qf, kf, vf, log_f)
        yf = merge_heads(of) @ w_branch_fx[l]

        mh = rms_norm(mem, mem_norm[l])
        m_k, m_v = jnp.split(mh @ w_mem_kv[l], 2, axis=-1)
        qm = rms_norm(split_heads(m_q, MEM_HEADS, MEM_DIM), mem_q_norm[l])
        km = rms_norm(split_heads(m_k, MEM_HEADS, MEM_DIM), mem_k_norm[l])
        vm = split_heads(m_v, MEM_HEADS, MEM_DIM)
        om = memory_attention(qm, km, vm)
        ym = merge_heads(om) @ w_branch_mem[l]

        gates = jax.nn.sigmoid(g_logit.reshape(b, s_len, N_BRANCH, D_MODEL) + b_gate[l])
        merged = gates[:, :, 0] * ya + gates[:, :, 1] * yf + gates[:, :, 2] * ym
        x = x + merged @ w_out[l]

        h2 = rms_norm(x, norm_ffn[l])
        u = h2 @ w_up[l]
        u_pad = jnp.pad(u, ((0, 0), (CONV_W - 1, 0), (0, 0)))
        cw = conv_w[l]
        uc = sum(u_pad[:, j:j + s_len] * cw[j] for j in range(CONV_W)) + conv_b[l]
        a, g = jnp.split(uc, 2, axis=-1)
        x = x + (jax.nn.silu(a) * g) @ w_down[l]
    return x
```

```python
import math
from contextlib import ExitStack

import numpy as np
import concourse.bass as bass
import concourse.mybir as mybir
from concourse.bass_utils import run_bass_kernel_spmd

F32 = mybir.dt.float32
BF16 = mybir.dt.bfloat16
U8 = mybir.dt.uint8
AF = mybir.ActivationFunctionType
ALU = mybir.AluOpType
AX = mybir.AxisListType

D = 1024
S = 2048
NT = S // 128
NMEM = 256
IN_COLS = 6664
DFF = 2816
NFF = DFF // 128
EPS = 1e-6
SLOPES = [2.0 ** (-8.0 * (i + 1) / 4) for i in range(4)]
LAM_INIT = 0.8 - 0.6 * math.exp(-0.3 * 0)
NEG = -30000.0
import os
OPT_INTERLEAVE = int(os.environ.get('OPT_INTERLEAVE', '1'))
MAX_INFLIGHT = int(os.environ.get('MAX_INFLIGHT', '6'))
STOP_AFTER = int(os.environ.get('STOP_AFTER', '99'))
NO_WAW_SKIP = int(os.environ.get('NO_WAW_SKIP', '0'))
UNITS_LIMIT = int(os.environ.get('UNITS_LIMIT', '12'))
SKIP_ATTN = int(os.environ.get('SKIP_ATTN', '0'))
PROJ_STAGE = int(os.environ.get('PROJ_STAGE', '9'))
DUMMY_MM = int(os.environ.get('DUMMY_MM', '0'))
O_AQ, O_AK, O_AV, O_FQ, O_FK, O_FV, O_FG, O_MQ, O_G = 0, 512, 1024, 1536, 2048, 2560, 3072, 3080, 3592

C_ID, C_U, C_ONES, C_BDA, C_BFX, C_ALK, C_DAQ, C_END = 0, 128, 256, 384, 896, 1024, 1088, 1152

PARAM_NAMES = ["norm_mix", "w_in", "b_gate", "da_q_norm", "da_k_norm", "da_lambda_q1", "da_lambda_k1",
               "da_lambda_q2", "da_lambda_k2", "da_subln", "fx_q_norm", "fx_k_norm", "fx_f_bias",
               "mem_norm", "w_mem_kv", "mem_q_norm", "mem_k_norm", "w_branch_da", "w_branch_fx",
               "w_branch_mem", "w_out", "norm_ffn", "w_up", "conv_w", "conv_b", "w_down"]


def make_consts():
    c = np.zeros((128, C_END), np.float32)
    p = np.arange(128)
    c[:, C_ID:C_ID + 128] = np.eye(128, dtype=np.float32)
    c[:, C_U:C_U + 128] = (p[:, None] <= p[None, :]).astype(np.float32)
    c[:, C_ONES:C_ONES + 128] = 1.0
    k = p[:, None]
    q = p[None, :]
    for h in range(4):
        b = np.where(k <= q, 0.0,
                     np.where((k // 64) == (q // 64), -2.0 * SLOPES[h] * (k - q), NEG))
        c[:, C_BDA + 128 * h:C_BDA + 128 * (h + 1)] = b
    c[:, C_BFX:C_BFX + 128] = np.where(k <= q, 0.0, NEG)
    for j in range(NT):
        for h in range(4):
            pos = 128 * j + p
            c[:, C_ALK + j * 4 + h] = SLOPES[h] * pos
            c[:, C_DAQ + j * 4 + h] = -SLOPES[h] * pos * 8.0
    return c


class DSem:
    def __init__(self, h):
        self.h = h
        self.count = 0


class Res:
    __slots__ = ("name", "w", "r", "dsem", "psum")

    def __init__(self, name, dsem=None):
        self.name = name
        self.psum = False
        self.w = None
        self.r = {}
        self.dsem = dsem


class K:
    def __init__(self, nc, es):
        self.nc = nc
        self.es = es
        self.eng = {"pe": nc.tensor, "act": nc.scalar, "dve": nc.vector, "pool": nc.gpsimd, "sp": nc.sync}
        self.sem = {}
        self.cnt = {}
        self.seen = {}
        for e in self.eng:
            self.sem[e] = es.enter_context(nc.semaphore("sem_" + e))
            self.cnt[e] = 0
            self.seen[e] = {}
        self.dsems = []
        self.nres = 0
        self.inflight = {}

    def res(self, name, dma=False):
        self.nres += 1
        ds = None
        if dma:
            ds = DSem(self.es.enter_context(self.nc.semaphore("d_%s_%d" % (name, self.nres))))
            self.dsems.append(ds)
        return Res(name, ds)

    def wait(self, e, tok):
        if tok is None:
            return
        sem, val = tok
        key = sem.num
        if self.seen[e].get(key, 0) >= val:
            return
        self.eng[e].wait_ge(sem, val)
        self.seen[e][key] = val

    def _deps(self, e, reads, writes, accum):
        for r in reads:
            self.wait(e, r.w)
            if r.psum:
                for t in list(r.r.values()):
                    if t[0] is not self.sem[e]:
                        self.wait(e, t)
        for w in writes:
            if not (accum and w.w is not None and w.w[0] is self.sem[e]):
                self.wait(e, w.w)
            for t in list(w.r.values()):
                self.wait(e, t)

    def _record(self, tok, reads, writes):
        for r in reads:
            old = r.r.get(tok[0].num)
            if old is None or old[1] < tok[1]:
                r.r[tok[0].num] = tok
        for w in writes:
            w.w = tok
            w.r = {}

    def op(self, e, fn, reads=(), writes=(), signal=True, accum=False):
        self._deps(e, reads, writes, accum)
        ins = fn(self.eng[e])
        if signal:
            self.cnt[e] += 1
            ins.then_inc(self.sem[e], 1)
            tok = (self.sem[e], self.cnt[e])
        else:
            tok = (self.sem[e], self.cnt[e] + 1)
        self._record(tok, reads, writes)
        return ins

    def dma(self, q, out, in_, reads=(), writes=(), sem_res=None, **kw):
        ds = sem_res.dsem
        for r in reads:
            self.wait(q, r.w)
        for w in writes:
            if NO_WAW_SKIP or not (w.w is not None and w.w[0] is ds.h):
                self.wait(q, w.w)
            for t in list(w.r.values()):
                self.wait(q, t)
        fl = self.inflight.setdefault(q, [])
        if len(fl) >= MAX_INFLIGHT:
            self.wait(q, fl.pop(0))
        ins = self.eng[q].dma_start(out=out, in_=in_, **kw)
        ds.count += 16
        ins.then_inc(ds.h, 16)
        tok = (ds.h, ds.count)
        fl.append(tok)
        self._record(tok, reads, writes)
        return ins

    def barrier(self):
        toks = [(self.sem[e], self.cnt[e]) for e in self.eng if self.cnt[e] > 0]
        toks += [(d.h, d.count) for d in self.dsems if d.count > 0]
        for e in self.eng:
            for t in toks:
                self.wait(e, t)

    def final(self):
        toks = [(d.h, d.count) for d in self.dsems if d.count > 0]
        toks += [(self.sem[e], self.cnt[e]) for e in self.eng if self.cnt[e] > 0]
        for t in toks:
            self.wait("sp", t)


class Bump:
    def __init__(self, arena, lo, hi):
        self.arena, self.lo, self.hi, self.cur = arena, lo, hi, lo

    def reset(self):
        self.cur = self.lo

    def alloc(self, shape, dt):
        n = int(np.prod(shape))
        sz = n * (4 if dt == F32 else 2)
        off = (self.cur + 31) // 32 * 32
        assert off + sz <= self.hi, ("arena overflow", off, sz, self.hi)
        self.cur = off + sz
        ap = self.arena[:, off:off + sz].bitcast(dt)
        if len(shape) == 2:
            ap = ap.rearrange("p (a b) -> p a b", a=shape[0])
        elif len(shape) == 3:
            ap = ap.rearrange("p (a b c) -> p a b c", a=shape[0], b=shape[1])
        return ap


def build(nseq=2, dbg=False):
    nc = bass.Bass("TRN2", target_bir_lowering=False)
    x_d = nc.dram_tensor("x", [nseq, S, D], F32, kind="ExternalInput").ap()
    mem_d = nc.dram_tensor("mem", [nseq, NMEM, D], F32, kind="ExternalInput").ap()
    cst_d = nc.dram_tensor("cst", [128, C_END], F32, kind="ExternalInput").ap()
    shp = {"norm_mix": [1, D], "w_in": [1, D, IN_COLS], "b_gate": [1, 3, D], "da_q_norm": [1, 64],
           "da_k_norm": [1, 64], "da_lambda_q1": [1, 64], "da_lambda_k1": [1, 64], "da_lambda_q2": [1, 64],
           "da_lambda_k2": [1, 64], "da_subln": [1, 128], "fx_q_norm": [1, 64], "fx_k_norm": [1, 64],
           "fx_f_bias": [1, 8], "mem_norm": [1, D], "w_mem_kv": [1, D, 1024], "mem_q_norm": [1, 128],
           "mem_k_norm": [1, 128], "w_branch_da": [1, 512, D], "w_branch_fx": [1, 512, D],
           "w_branch_mem": [1, 512, D], "w_out": [1, D, D], "norm_ffn": [1, D], "w_up": [1, D, 2 * DFF],
           "conv_w": [1, 3, 2 * DFF], "conv_b": [1, 2 * DFF], "w_down": [1, DFF, D]}
    P = {n: nc.dram_tensor(n, shp[n], F32, kind="ExternalInput").ap() for n in PARAM_NAMES}
    out_d = nc.dram_tensor("out", [nseq, S, D], F32, kind="ExternalOutput").ap()
    wup_s = nc.dram_tensor("wup_s", [2 * NFF, 128, 1024], BF16).ap()
    dbg_d = nc.dram_tensor("dbg", [128, 8192], F32, kind="ExternalOutput").ap() if dbg else None

    WQ = "pool"
    AQ = "sp"

    with ExitStack() as es:
        ARENA = 207 * 1024
        arena = es.enter_context(nc.sbuf_tensor("arena", [128, ARENA], U8))
        psb = [es.enter_context(nc.psum_tensor("ps%d" % i, [128, 512], F32)) for i in range(8)]
        k = K(nc, es)
        PS = [k.res("ps%d" % i) for i in range(8)]
        for r_ in PS:
            r_.psum = True

        def psf(i):
            return psb[i][:, :]

        def psh(i):
            return psb[i][:, :].bitcast(BF16)

        CB = Bump(arena, 0, 16 * 1024)
        HT0 = 16 * 1024
        RA0 = HT0 + 32 * 1024
        RB0 = RA0 + 64 * 1024
        RC0 = RB0 + 32 * 1024
        hT = Bump(arena, HT0, RA0).alloc([8, S], BF16)
        RA = Bump(arena, RA0, RB0)
        RB = Bump(arena, RB0, RC0)
        RC = Bump(arena, RC0, ARENA)
        hT_res = [k.res("hT%d" % t) for t in range(NT)]
        r_wups = [k.res("wups%d" % p, dma=True) for p in range(NFF)]

        cst = CB.alloc([C_END], F32)
        r_cst = k.res("cst", dma=True)
        k.dma(AQ, cst, cst_d, writes=[r_cst], sem_res=r_cst)
        identb = CB.alloc([128], BF16)
        r_idb = k.res("identb", dma=True)
        k.dma(WQ, identb, cst_d[:, C_ID:C_ID + 128], writes=[r_idb], sem_res=r_idb)
        identf = cst[:, C_ID:C_ID + 128]
        Umat = cst[:, C_U:C_U + 128]
        ones_f = cst[:, C_ONES:C_ONES + 128]

        r_par = k.res("params", dma=True)

        def bload(name, n, reps=1):
            if isinstance(name, str):
                name = [name] * reps
            t = CB.alloc([n * len(name)], F32)
            for r, nm in enumerate(name):
                k.dma(AQ, t[:, r * n:(r + 1) * n], P[nm][0:1, :].partition_broadcast(128),
                      writes=[r_par], sem_res=r_par)
            return t

        g_da4 = bload(["da_q_norm", "da_q_norm", "da_k_norm", "da_k_norm"], 64)
        g_fx4 = bload(["fx_q_norm", "fx_q_norm", "fx_k_norm", "fx_k_norm"], 64)
        g_mq = bload("mem_q_norm", 128)
        g_mk4 = bload("mem_k_norm", 128, 4)
        g_sub = bload("da_subln", 128)
        fbias = bload("fx_f_bias", 8)
        lq1 = bload("da_lambda_q1", 64)
        lk1 = bload("da_lambda_k1", 64)
        lq2 = bload("da_lambda_q2", 64)
        lk2 = bload("da_lambda_k2", 64)

        rows1 = CB.alloc([128], F32)
        rows2 = CB.alloc([128], F32)
        r_rows = k.res("rows", dma=True)
        k.dma(AQ, rows1[0:8, :], P["norm_mix"][0].rearrange("(c p) -> c p", p=128), writes=[r_rows], sem_res=r_rows)
        k.dma(AQ, rows1[8:16, :], P["norm_ffn"][0].rearrange("(c p) -> c p", p=128), writes=[r_rows], sem_res=r_rows)
        k.dma(AQ, rows1[16:24, :], P["mem_norm"][0].rearrange("(c p) -> c p", p=128), writes=[r_rows], sem_res=r_rows)
        k.dma(AQ, rows1[24:68, :], P["conv_b"][0].rearrange("(c p) -> c p", p=128), writes=[r_rows], sem_res=r_rows)
        k.dma(AQ, rows1[68:112, :], P["conv_w"][0, 2].rearrange("(c p) -> c p", p=128), writes=[r_rows], sem_res=r_rows)
        k.dma(AQ, rows2[0:44, :], P["conv_w"][0, 0].rearrange("(c p) -> c p", p=128), writes=[r_rows], sem_res=r_rows)
        k.dma(AQ, rows2[44:88, :], P["conv_w"][0, 1].rearrange("(c p) -> c p", p=128), writes=[r_rows], sem_res=r_rows)
        pcol = CB.alloc([200], F32)
        r_pcol = k.res("pcol")
        k.op("pe", lambda e: e.matmul(psf(0)[:, 0:112], rows1[0:112, :], identf[0:112, 0:112], start=True, stop=True),
             reads=[r_rows, r_cst], writes=[PS[0]])
        k.op("pe", lambda e: e.matmul(psf(0)[:, 112:200], rows2[0:88, :], identf[0:88, 0:88], start=True, stop=True),
             reads=[r_rows, r_cst], writes=[PS[0]], accum=True)
        k.op("dve", lambda e: e.tensor_copy(pcol, psf(0)[:, 0:200]), reads=[PS[0]], writes=[r_pcol])
        nm_c, nf_c, mn_c = pcol[:, 0:8], pcol[:, 8:16], pcol[:, 16:24]

        lam_t = CB.alloc([8], F32)
        r_lam = k.res("lam")
        junk = CB.alloc([64], F32)
        r_junk = k.res("junk")
        k.op("dve", lambda e: e.tensor_tensor(junk, lq1, lk1, ALU.mult), reads=[r_par], writes=[r_junk])
        k.op("dve", lambda e: e.tensor_reduce(lam_t[:, 0:1], junk, AX.X, ALU.add), reads=[r_junk], writes=[r_lam])
        k.op("dve", lambda e: e.tensor_tensor(junk, lq2, lk2, ALU.mult), reads=[r_par, r_lam], writes=[r_junk])
        k.op("dve", lambda e: e.tensor_reduce(lam_t[:, 1:2], junk, AX.X, ALU.add), reads=[r_junk], writes=[r_lam])
        k.op("act", lambda e: e.activation(lam_t[:, 2:4], lam_t[:, 0:2], AF.Exp), reads=[r_lam], writes=[r_lam])
        k.op("dve", lambda e: e.tensor_tensor(lam_t[:, 4:5], lam_t[:, 2:3], lam_t[:, 3:4], ALU.subtract),
             reads=[r_lam], writes=[r_lam])
        k.op("dve", lambda e: e.tensor_scalar(lam_t[:, 5:6], lam_t[:, 4:5], -1.0, -LAM_INIT, ALU.mult, ALU.add),
             reads=[r_lam], writes=[r_lam])
        neg_lam = lam_t[:, 5:6]
        k.op("dve", lambda e: e.tensor_scalar(g_sub, g_sub, 1.0 - LAM_INIT, None, ALU.mult),
             reads=[r_par], writes=[r_par])
        zall = CB.alloc([NT, 8], F32)
        logf = CB.alloc([NT, 8], F32)
        negc = CB.alloc([NT, 8], F32)
        c8 = CB.alloc([NT, 8], F32)
        r_z, r_logf, r_negc, r_c8 = k.res("z"), k.res("logf"), k.res("negc"), k.res("c8")
        pref = zall
        r_pref = r_z
        small = CB.alloc([128], F32)
        r_small = [k.res("small%d" % i) for i in range(16)]
        small_i = [0]

        def get_small():
            i = small_i[0] % 16
            small_i[0] += 1
            return small[:, i * 8:(i + 1) * 8], r_small[i]

        def rstd_from_ss(ss_ap, r_ss, n, inv_d):
            k.op("act", lambda e: e.activation(ss_ap, ss_ap, AF.Ln, bias=eps_c, scale=inv_d),
                 reads=[r_ss, r_par], writes=[r_ss])
            k.op("act", lambda e: e.activation(ss_ap, ss_ap, AF.Exp, scale=-0.5), reads=[r_ss], writes=[r_ss])

        eps_c = CB.alloc([1], F32)
        k.op("pool", lambda e: e.memset(eps_c, EPS), writes=[r_par])

        rss = CB.alloc([8], F32)
        r_rss = [k.res("rss%d" % i) for i in range(4)]

        def rms_s1(t, src_ap, r_src, sq_ap, r_sq):
            ss = rss[:, t % 4:t % 4 + 1]
            k.op("act", lambda e: e.activation(sq_ap, src_ap, AF.Square, accum_out=ss),
                 reads=[r_src], writes=[r_sq, r_rss[t % 4]])

        def rms_s2(t, src_ap, r_src, xn_ap, r_xn):
            ss = rss[:, t % 4:t % 4 + 1]
            rstd_from_ss(ss, r_rss[t % 4], 1, 1.0 / D)
            k.op("act", lambda e: e.activation(xn_ap, src_ap, AF.Copy, scale=ss),
                 reads=[r_src, r_rss[t % 4]], writes=[r_xn])

        def rms_s3(xn_ap, r_xn, gain_cols, dstT, r_dst, pbank):
            for c in range(8):
                k.op("pe", lambda e, c=c: e.transpose(psh(pbank)[:, c * 128:(c + 1) * 128],
                                                      xn_ap[:, c * 128:(c + 1) * 128], identb),
                     reads=[r_xn, r_idb], writes=[PS[pbank]], signal=(c == 7), accum=True)
            k.op("dve", lambda e: e.tensor_tensor(dstT, psh(pbank).rearrange("p (c t) -> p c t", c=8),
                                                  gain_cols.unsqueeze(2).broadcast_to([128, 8, 128]), ALU.mult),
                 reads=[PS[pbank], r_pcol], writes=[r_dst])

        def rms_to_T(src_ap, r_src, xn_ap, r_xn, sq_ap, r_sq, gain_cols, dstT, r_dst, pbank, t=0):
            rms_s1(t, src_ap, r_src, sq_ap, r_sq)
            rms_s2(t, src_ap, r_src, xn_ap, r_xn)
            rms_s3(xn_ap, r_xn, gain_cols, dstT, r_dst, pbank)

        def qknorm(ps_ap, r_ps, H, d, gain_ap, out_ap, r_out, sq_ap, r_sq, tmp_ap, r_tmp):
            ss, r_ss = get_small()
            k.op("act", lambda e: e.activation(sq_ap[:, 0:H * d], ps_ap, AF.Square), reads=[r_ps], writes=[r_sq])
            k.op("dve", lambda e: e.tensor_reduce(ss[:, 0:H], sq_ap[:, 0:H * d].rearrange("p (h d) -> p h d", h=H),
                                                  AX.X, ALU.add), reads=[r_sq], writes=[r_ss])
            rstd_from_ss(ss[:, 0:H], r_ss, H, 1.0 / d)
            k.op("dve", lambda e: e.tensor_tensor(tmp_ap[:, 0:H * d].rearrange("p (h d) -> p h d", h=H),
                                                  ps_ap.rearrange("p (h d) -> p h d", h=H),
                                                  ss[:, 0:H].unsqueeze(2).broadcast_to([128, H, d]), ALU.mult),
                 reads=[r_ps, r_ss], writes=[r_tmp])
            k.op("dve", lambda e: e.tensor_tensor(out_ap, tmp_ap[:, 0:H * d].rearrange("p (h d) -> p h d", h=H),
                                                  gain_ap.rearrange("p (h d) -> p h d", h=H), ALU.mult),
                 reads=[r_tmp, r_par], writes=[r_out])

        def wload(dst, r_dst, src, eng=None):
            k.dma(eng or WQ, dst, src, writes=[r_dst], sem_res=r_dst)

        def win_cols(c0, n):
            return P["w_in"][0, :, c0:c0 + n].rearrange("(c p) n -> p c n", p=128)

        dbg_n = [0]

        def dump(ap, reads, ncols):
            if not dbg:
                return
            r = k.res("dbg", dma=True)
            k.dma("pool", dbg_d[:, dbg_n[0]:dbg_n[0] + ncols], ap, reads=reads, writes=[r], sem_res=r)
            dbg_n[0] += ncols

        for s in range(nseq):
            RA.reset(); RB.reset(); RC.reset()
            o_br = [RA.alloc([NT, 512], BF16) for _ in range(3)]
            r_obr = [[k.res("o%d_%d" % (b, t)) for t in range(NT)] for b in range(3)]
            xt = [RA.alloc([D], F32) for _ in range(2)]
            r_xt = [k.res("xt%d" % i, dma=True) for i in range(2)]
            xn = [RA.alloc([D], BF16) for _ in range(2)]
            r_xn = [k.res("xn%d" % i) for i in range(2)]
            wkv = Bump(arena, RC0 + 13 * 1024, ARENA).alloc([8, 1024], BF16)
            r_wkv = k.res("wkv", dma=True)
            mhT = RB.alloc([8, NMEM], BF16)
            r_mhT = [k.res("mhT%d" % i) for i in range(2)]
            kmT = RB.alloc([4, NMEM], BF16)
            r_kmT = k.res("kmT")
            Vm = RB.alloc([2, 4 * 129], BF16)
            r_Vm = k.res("Vm")
            kbm = RB.alloc([4, 128], BF16)
            r_kbm = k.res("kbm")
            wu = [RC.alloc([8, 384], BF16) for _ in range(2)]
            r_wu = [k.res("wu%d" % i, dma=True) for i in range(2)]
            qkT = [RC.alloc([4, S], BF16) for _ in range(2)]
            Va = [RC.alloc([NT, 130], BF16) for _ in range(2)]
            r_qT = [[k.res("qkT%d_%d" % (i, t)) for t in range(NT)] for i in range(2)]
            r_Va = [[k.res("Va%d_%d" % (i, t)) for t in range(NT)] for i in range(2)]
            qka = [RA.alloc([4, 65], BF16) for _ in range(2)]
            r_qka = [k.res("qka%d" % i) for i in range(2)]
            qmem = [RA.alloc([128], BF16) for _ in range(2)]
            praw = [RA.alloc([256], F32) for _ in range(2)]
            r_praw = [k.res("praw%d" % i) for i in range(2)]
            sqj = RB.alloc([D], F32)
            r_sq512 = k.res("sqj")
            r_tmp512 = r_sq512
            sq512 = sqj[:, 0:512]
            tmp512 = sqj[:, 512:1024]
            sqb = [sqj]
            r_sqb = [r_sq512]
            sqs = [RB.alloc([256], F32) for _ in range(2)]
            r_sqs = [k.res("sqs%d" % i) for i in range(2)]
            tmps = [RB.alloc([256], F32) for _ in range(2)]
            r_tmps = [k.res("tmps%d" % i) for i in range(2)]
            PT = [RB.alloc([512], BF16) for _ in range(4)]
            r_PT = [k.res("PT%d" % i) for i in range(4)]
            tdiag = [RB.alloc([128], F32) for _ in range(2)]
            r_tdiag = [k.res("tdiag%d" % i) for i in range(2)]
            O0n = RB.alloc([NT, 128], F32)
            r_O0n = [k.res("O0n%d" % i) for i in range(NT)]
            finjunk = RB.alloc([128], F32)
            pss = RB.alloc([24], F32)
            r_pss = [k.res("pss%d" % i) for i in range(3)]
            fss = RB.alloc([16], F32)
            r_fss = [k.res("fss%d" % i) for i in range(16)]
            r_finjunk = k.res("finjunk")
            wfg = RC.alloc([8, 8], BF16)
            r_wfg = k.res("wfg", dma=True)

            wload(wkv, r_wkv, P["w_mem_kv"][0].rearrange("(c p) n -> p c n", p=128))
            wload(wfg, r_wfg, win_cols(O_FG, 8))
            k.op("pool", lambda e: e.memset(Vm.rearrange("p a (h d) -> p a h d", h=4)[:, :, :, 128:129], 1.0),
                 writes=[r_Vm])
            for i_ in range(2):
                k.op("pool", lambda e, i_=i_: e.memset(qka[i_][:, 2:4, 64:65], 1.0), writes=[r_qka[i_]])
            for mt in range(2):
                sl = mt % 2
                k.dma(AQ, xt[sl], mem_d[s, mt * 128:(mt + 1) * 128, :], writes=[r_xt[sl]], sem_res=r_xt[sl])
                rms_to_T(xt[sl], r_xt[sl], xn[sl], r_xn[sl], sqb[0], r_sqb[0], mn_c,
                         mhT[:, :, mt * 128:(mt + 1) * 128], r_mhT[mt], 7)
            for mt in range(2):
                for half in range(2):
                    pb = 5 + half
                    for c in range(8):
                        k.op("pe", lambda e, c=c, pb=pb, half=half, mt=mt: e.matmul(
                            psf(pb), mhT[:, c, mt * 128:(mt + 1) * 128], wkv[:, c, half * 512:(half + 1) * 512],
                            start=(c == 0), stop=(c == 7)),
                             reads=[r_mhT[mt], r_wkv], writes=[PS[pb]], signal=(c == 7), accum=True)
                    if half == 0:
                        qknorm(psf(pb), PS[pb], 4, 128, g_mk4, kbm, r_kbm, sq512, r_sq512, tmp512, r_tmp512)
                        for h in range(4):
                            k.op("pe", lambda e, h=h: e.transpose(psh(7)[:, h * 128:(h + 1) * 128], kbm[:, h, :], identb),
                                 reads=[r_kbm, r_idb], writes=[PS[7]], signal=(h == 3), accum=True)
                        k.op("act", lambda e, mt=mt: e.copy(kmT[:, :, mt * 128:(mt + 1) * 128],
                                                            psh(7)[:, 0:512].rearrange("p (h t) -> p h t", h=4)),
                             reads=[PS[7]], writes=[r_kmT])
                    else:
                        k.op("act", lambda e, mt=mt, pb=pb: e.copy(
                            Vm[:, mt, :].rearrange("p (h d) -> p h d", h=4)[:, :, 0:128],
                            psf(pb).rearrange("p (h d) -> p h d", h=4)), reads=[PS[pb]], writes=[r_Vm])

            for n_ in range(NT + 2):
                if n_ - 2 >= 0:
                    t = n_ - 2
                    rms_s3(xn[t % 2], r_xn[t % 2], nm_c, hT[:, :, t * 128:(t + 1) * 128], hT_res[t], 6 + t % 2)
                if 0 <= n_ - 1 < NT:
                    t = n_ - 1
                    rms_s2(t, xt[t % 2], r_xt[t % 2], xn[t % 2], r_xn[t % 2])
                if n_ < NT:
                    t = n_
                    k.dma(AQ, xt[t % 2], x_d[s, t * 128:(t + 1) * 128, :], writes=[r_xt[t % 2]], sem_res=r_xt[t % 2])
                    rms_s1(t, xt[t % 2], r_xt[t % 2], sqb[0], r_sqb[0])

            for t in range(NT):
                for c in range(8):
                    k.op("pe", lambda e, c=c, t=t: e.matmul(psf(6)[:, t * 8:(t + 1) * 8], hT[:, c, t * 128:(t + 1) * 128],
                                                           wfg[:, c, :], start=(c == 0), stop=(c == 7)),
                         reads=[hT_res[t], r_wfg], writes=[PS[6]], signal=(c == 7 and t == NT - 1), accum=True)
            k.op("dve", lambda e: e.tensor_tensor(zall, psf(6)[:, 0:NT * 8].rearrange("p (t h) -> p t h", t=NT),
                                                  fbias.unsqueeze(1).broadcast_to([128, NT, 8]), ALU.add),
                 reads=[PS[6], r_par], writes=[r_z])
            k.op("act", lambda e: e.activation(zall, zall, AF.Exp, scale=-1.0), reads=[r_z], writes=[r_z])
            k.op("act", lambda e: e.activation(zall, zall, AF.Ln, bias=1.0), reads=[r_z], writes=[r_z])
            k.op("dve", lambda e: e.tensor_scalar(logf, zall, -1.0, None, ALU.mult), reads=[r_z], writes=[r_logf])
            k.op("pe", lambda e: e.matmul(psf(5)[:, 0:NT * 8], ones_f, logf.rearrange("p t h -> p (t h)"),
                                          start=True, stop=True), reads=[r_logf, r_cst], writes=[PS[5]])
            k.op("dve", lambda e: e.memset(pref[:, 0, :], 0.0), writes=[r_pref])
            for t in range(1, NT):
                k.op("dve", lambda e, t=t: e.tensor_tensor(pref[:, t, :], pref[:, t - 1, :], psf(5)[:, (t - 1) * 8:t * 8], ALU.add),
                     reads=[PS[5], r_pref], writes=[r_pref])
            for t in range(NT):
                k.op("pe", lambda e, t=t: e.matmul(psf(6)[:, t * 8:(t + 1) * 8], Umat, logf[:, t, :], start=True, stop=True),
                     reads=[r_logf, r_cst], writes=[PS[6]], signal=(t == NT - 1), accum=True)
            k.op("dve", lambda e: e.tensor_tensor(pref, pref, psf(6)[:, 0:NT * 8].rearrange("p (t h) -> p t h", t=NT), ALU.add),
                 reads=[PS[6], r_pref], writes=[r_pref])
            k.op("dve", lambda e: e.tensor_scalar(negc, pref, -1.0, None, ALU.mult), reads=[r_pref], writes=[r_negc])
            k.op("dve", lambda e: e.tensor_scalar(c8, pref, 8.0, None, ALU.mult), reads=[r_pref], writes=[r_c8])

            k.barrier()

            pt_i = [0]
            st_i = [0]
            td_i = [0]

            def attention(qT_of, r_q, kT_of, r_k, nkt_of_g, causal, bias_of, Bmat, V_of, r_V, dvp1, scale, finish):
                for g in range(4):
                    nk = nkt_of_g(g)
                    pend = None
                    for j in range(nk + 1):
                        if j < nk:
                            col0 = 128 * (j - 4 * g) if (causal and j >= 4 * g) else 0
                            sb = 4 + (st_i[0] % 2)
                            st_i[0] += 1
                            q_r = [r_q[t] for t in range(4 * g + col0 // 128, 4 * g + 4)]
                            k.op("pe", lambda e, sb=sb, j=j, col0=col0, g=g: e.matmul(
                                psf(sb)[:, col0:512], kT_of(j), qT_of(512 * g + col0, 512 * (g + 1)),
                                start=True, stop=True), reads=q_r + [r_k[j]], writes=[PS[sb]])
                            pi = pt_i[0] % 4
                            pt_i[0] += 1
                            c1 = col0
                            if causal and j >= 4 * g:
                                di = td_i[0] % 2
                                td_i[0] += 1
                                k.op("dve", lambda e, sb=sb, col0=col0, di=di: e.scalar_tensor_tensor(
                                    tdiag[di], psf(sb)[:, col0:col0 + 128], scale, Bmat, ALU.mult, ALU.add),
                                     reads=[PS[sb], r_cst], writes=[r_tdiag[di]])
                                k.op("act", lambda e, di=di, pi=pi, col0=col0, j=j: e.activation(
                                    PT[pi][:, col0:col0 + 128], tdiag[di], AF.Exp, bias=bias_of(j), scale=1.0),
                                     reads=[r_tdiag[di], r_negc, r_cst], writes=[r_PT[pi]])
                                c1 = col0 + 128
                            if c1 < 512:
                                if bias_of is not None:
                                    k.op("act", lambda e, sb=sb, pi=pi, c1=c1, j=j: e.activation(
                                        PT[pi][:, c1:512], psf(sb)[:, c1:512], AF.Exp, bias=bias_of(j), scale=scale),
                                         reads=[PS[sb], r_negc, r_cst], writes=[r_PT[pi]])
                                else:
                                    k.op("act", lambda e, sb=sb, pi=pi, c1=c1: e.activation(
                                        PT[pi][:, c1:512], psf(sb)[:, c1:512], AF.Exp, scale=scale),
                                         reads=[PS[sb]], writes=[r_PT[pi]])
                            cur = (j, pi, col0)
                        else:
                            cur = None
                        if pend is not None:
                            jj, pi2, col02 = pend
                            had_fin = False
                            for i in range(col02 // 128, 4):
                                last = (jj == 4 * g + i) if causal else (jj == nk - 1)
                                k.op("pe", lambda e, i=i, pi2=pi2, jj=jj, last=last: e.matmul(
                                    psf(i)[:, 0:dvp1], PT[pi2][:, i * 128:(i + 1) * 128], V_of(jj),
                                    start=(jj == 0), stop=last),
                                     reads=[r_PT[pi2], r_V[jj]], writes=[PS[i]], accum=(jj != 0), signal=(i == 3))
                                if last:
                                    finish(4 * g + i, i)
                                    had_fin = True
                            if DUMMY_MM and not had_fin:
                                for b_ in range(4 - DUMMY_MM, 4):
                                    k.op("pe", lambda e, b_=b_: e.matmul(
                                        psf(b_)[:, 256:512], identb, PT[pi2][:, 0:256], start=False, stop=False,
                                        skip_group_check=True),
                                         reads=[r_PT[pi2], r_idb], writes=[PS[b_]], signal=False, accum=True)
                        pend = cur
                        tick()
                        yield

            units = [("da", h) for h in range(4)] + [("fx", p) for p in range(4)] + [("mem", h) for h in range(4)]
            units = units[:UNITS_LIMIT]

            def proj_gen(ui):
                kind, idx = units[ui]
                ub = ui % 2
                wsl = ui % 2
                if kind == "da":
                    offs = (O_AQ, O_AK, O_AV)
                elif kind == "fx":
                    offs = (O_FQ, O_FK, O_FV)
                else:
                    offs = (O_MQ,)
                for oi_, o_ in enumerate(offs):
                    wload(wu[wsl][:, :, oi_ * 128:(oi_ + 1) * 128], r_wu[wsl], win_cols(o_ + 128 * idx, 128))
                ncol = 128 * len(offs)
                if s == 0:
                    for p_ in range(2 * ui, min(2 * ui + 2, NFF)):
                        for part_ in range(2):
                            k.dma(WQ, wup_s[2 * p_ + part_].rearrange("p (c n) -> p c n", c=8),
                                  P["w_up"][0, :, part_ * DFF + 128 * p_:part_ * DFF + 128 * (p_ + 1)].rearrange(
                                      "(c p) n -> p c n", p=128), writes=[r_wups[p_]], sem_res=r_wups[p_])
                if kind == "da":
                    k.op("pool", lambda e: e.memset(Va[ub][:, :, 128:129], 1.0), writes=r_Va[ub])
                elif kind == "fx":
                    k.op("pool", lambda e: e.memset(
                        Va[ub].rearrange("p t (h d) -> p t h d", h=2)[:, :, :, 64:65], 1.0), writes=r_Va[ub])

                def tail(t):
                    qs = t % 2
                    if kind in ("da", "fx"):
                        for m in range(4):
                            k.op("pe", lambda e, m=m: e.transpose(psh(7)[0:65, m * 128:(m + 1) * 128],
                                                                   qka[qs][:, m, :], identb),
                                 reads=[r_qka[qs], r_idb], writes=[PS[7]], signal=(m == 3), accum=True)
                        k.op("dve", lambda e: e.tensor_copy(qkT[ub][0:65, :, t * 128:(t + 1) * 128],
                                                            psh(7)[0:65, 0:512].rearrange("p (m t) -> p m t", m=4)),
                             reads=[PS[7]], writes=[r_qT[ub][t]])
                    else:
                        k.op("pe", lambda e: e.transpose(psh(7)[:, 0:128], qmem[qs], identb),
                             reads=[r_qka[qs], r_idb], writes=[PS[7]])
                        k.op("act", lambda e: e.copy(qkT[ub][:, 0, t * 128:(t + 1) * 128], psh(7)[:, 0:128]),
                             reads=[PS[7]], writes=[r_qT[ub][t]])

                nqk = 256 if kind != "mem" else 128
                H, dd = (4, 64) if kind != "mem" else (1, 128)

                def stepA(t):
                    pb = 6
                    qs = t % 2
                    for c in range(8):
                        k.op("pe", lambda e, c=c: e.matmul(
                            psf(pb)[:, 0:ncol], hT[:, c, t * 128:(t + 1) * 128], wu[wsl][:, c, 0:ncol],
                            start=(c == 0), stop=(c == 7)),
                             reads=[hT_res[t], r_wu[wsl]], writes=[PS[pb]], signal=(c == 7), accum=True)

                def stepA2(t):
                    pb = 6
                    qs = t % 2
                    k.op("dve", lambda e: e.tensor_copy(praw[qs][:, 0:nqk], psf(pb)[:, 0:nqk]),
                         reads=[PS[pb]], writes=[r_praw[qs]])
                    if kind == "da":
                        k.op("act", lambda e: e.copy(Va[ub][:, t, 0:128], psf(pb)[:, 256:384]),
                             reads=[PS[pb]], writes=[r_Va[ub][t]])
                    elif kind == "fx":
                        k.op("act", lambda e: e.copy(
                            Va[ub][:, t, :].rearrange("p (h d) -> p h d", h=2)[:, :, 0:64],
                            psf(pb)[:, 256:384].rearrange("p (h d) -> p h d", h=2)),
                             reads=[PS[pb]], writes=[r_Va[ub][t]])
                    ss = pss[:, (t % 3) * 8:(t % 3) * 8 + 8]
                    r_ss = r_pss[t % 3]
                    k.op("act", lambda e: e.activation(sqs[qs][:, 0:nqk], praw[qs][:, 0:nqk], AF.Square),
                         reads=[r_praw[qs]], writes=[r_sqs[qs]])
                    k.op("dve", lambda e: e.tensor_reduce(ss[:, 0:H], sqs[qs][:, 0:nqk].rearrange("p (h d) -> p h d", h=H),
                                                          AX.X, ALU.add), reads=[r_sqs[qs]], writes=[r_ss])

                def stepB(t):
                    qs = t % 2
                    ss = pss[:, (t % 3) * 8:(t % 3) * 8 + 8]
                    r_ss = r_pss[t % 3]
                    rstd_from_ss(ss[:, 0:H], r_ss, H, 1.0 / dd)
                    k.op("dve", lambda e: e.tensor_tensor(
                        tmps[qs][:, 0:nqk].rearrange("p (h d) -> p h d", h=H),
                        praw[qs][:, 0:nqk].rearrange("p (h d) -> p h d", h=H),
                        ss[:, 0:H].unsqueeze(2).broadcast_to([128, H, dd]), ALU.mult),
                         reads=[r_praw[qs], r_ss], writes=[r_tmps[qs]])
                    if kind != "mem":
                        gt = g_da4 if kind == "da" else g_fx4
                        k.op("dve", lambda e: e.tensor_tensor(
                            qka[qs][:, :, 0:64], tmps[qs][:, 0:256].rearrange("p (h d) -> p h d", h=4),
                            gt.rearrange("p (h d) -> p h d", h=4), ALU.mult),
                             reads=[r_tmps[qs], r_par], writes=[r_qka[qs]])
                        if kind == "da":
                            k.op("dve", lambda e: e.tensor_copy(
                                qka[qs][:, 0:2, 64:65],
                                cst[:, C_DAQ + t * 4 + idx:C_DAQ + t * 4 + idx + 1].unsqueeze(1).broadcast_to([128, 2, 1])),
                                 reads=[r_cst], writes=[r_qka[qs]])
                        else:
                            k.op("dve", lambda e: e.tensor_copy(
                                qka[qs][:, 0:2, 64:65], c8[:, t, 2 * idx:2 * idx + 2].unsqueeze(2)),
                                 reads=[r_c8], writes=[r_qka[qs]])
                    else:
                        k.op("dve", lambda e: e.tensor_tensor(
                            qmem[qs], tmps[qs][:, 0:128], g_mq, ALU.mult),
                             reads=[r_tmps[qs], r_par], writes=[r_qka[qs]])

                for n_ in range(NT + 3):
                    if n_ - 3 >= 0:
                        tail(n_ - 3)
                    if 0 <= n_ - 2 < NT:
                        stepB(n_ - 2)
                    if 0 <= n_ - 1 < NT:
                        stepA2(n_ - 1)
                    yield
                    if n_ < NT:
                        stepA(n_)
                    yield

            fss_i = [0]
            deferred = []

            def defer(n, fn):
                deferred.append([n, fn])

            def tick(flush=False):
                for d_ in list(deferred):
                    d_[0] -= 1
                    if d_[0] <= 0 or flush:
                        deferred.remove(d_)
                        d_[1]()

            def attn_gen(ui):
                kind, idx = units[ui]
                ub = ui % 2
                if kind == "da":
                    h = idx
                    for m in range(2):
                        def fin(tile, i, m=m, h=h):
                            rc, r_rc = get_small()
                            k.op("dve", lambda e: e.reciprocal(rc[:, 0:1], psf(i)[:, 128:129]),
                                 reads=[PS[i]], writes=[r_rc])
                            if m == 0:
                                k.op("dve", lambda e: e.tensor_scalar(O0n[:, tile, :], psf(i)[:, 0:128], rc[:, 0:1], None, ALU.mult),
                                     reads=[PS[i], r_rc], writes=[r_O0n[tile]])
                            else:
                                k.op("dve", lambda e: e.tensor_tensor(rc[:, 1:2], rc[:, 0:1], neg_lam, ALU.mult),
                                     reads=[r_rc, r_lam], writes=[r_rc])
                                k.op("dve", lambda e: e.scalar_tensor_tensor(
                                    O0n[:, tile, :], psf(i)[:, 0:128], rc[:, 1:2], O0n[:, tile, :], ALU.mult, ALU.add),
                                     reads=[PS[i], r_rc], writes=[r_O0n[tile]])
                                fi = fss_i[0] % 16
                                fss_i[0] += 1

                                def stB1():
                                    k.op("act", lambda e: e.activation(finjunk, O0n[:, tile, :], AF.Square, accum_out=fss[:, fi:fi + 1]),
                                         reads=[r_O0n[tile]], writes=[r_finjunk, r_fss[fi]])
                                    rstd_from_ss(fss[:, fi:fi + 1], r_fss[fi], 1, 1.0 / 128)

                                def stB2():
                                    k.op("dve", lambda e: e.scalar_tensor_tensor(
                                        o_br[0][:, tile, h * 128:(h + 1) * 128], O0n[:, tile, :], fss[:, fi:fi + 1], g_sub,
                                        ALU.mult, ALU.mult),
                                         reads=[r_O0n[tile], r_fss[fi], r_par], writes=[r_obr[0][tile]])
                                defer(2, stB1)
                                defer(4, stB2)
                        yield from attention(lambda c0, c1, m=m: qkT[ub][0:65, m, c0:c1], r_qT[ub],
                                             lambda j, m=m: qkT[ub][0:65, 2 + m, j * 128:(j + 1) * 128], r_qT[ub],
                                             lambda g: 4 * g + 4, True,
                                             lambda j, h=h: cst[:, C_ALK + j * 4 + h:C_ALK + j * 4 + h + 1],
                                             cst[:, C_BDA + 128 * h:C_BDA + 128 * (h + 1)],
                                             lambda j: Va[ub][:, j, 0:129], r_Va[ub], 129, 0.125, fin)
                    tick(flush=True)
                elif kind == "fx":
                    for m in range(2):
                        hd = 2 * idx + m

                        def fin(tile, i, hd=hd):
                            rc, r_rc = get_small()
                            k.op("dve", lambda e: e.reciprocal(rc[:, 0:1], psf(i)[:, 64:65]), reads=[PS[i]], writes=[r_rc])
                            k.op("dve", lambda e: e.tensor_scalar(o_br[1][:, tile, hd * 64:(hd + 1) * 64], psf(i)[:, 0:64],
                                                                  rc[:, 0:1], None, ALU.mult),
                                 reads=[PS[i], r_rc], writes=[r_obr[1][tile]])
                        yield from attention(lambda c0, c1, m=m: qkT[ub][0:65, m, c0:c1], r_qT[ub],
                                             lambda j, m=m: qkT[ub][0:65, 2 + m, j * 128:(j + 1) * 128], r_qT[ub],
                                             lambda g: 4 * g + 4, True,
                                             lambda j, hd=hd: negc[:, j, hd:hd + 1],
                                             cst[:, C_BFX:C_BFX + 128],
                                             lambda j, m=m: Va[ub][:, j, m * 65:(m + 1) * 65], r_Va[ub], 65, 0.125, fin)
                else:
                    h = idx

                    def fin(tile, i, h=h):
                        rc, r_rc = get_small()
                        k.op("dve", lambda e: e.reciprocal(rc[:, 0:1], psf(i)[:, 128:129]), reads=[PS[i]], writes=[r_rc])
                        k.op("dve", lambda e: e.tensor_scalar(o_br[2][:, tile, h * 128:(h + 1) * 128], psf(i)[:, 0:128],
                                                              rc[:, 0:1], None, ALU.mult),
                             reads=[PS[i], r_rc], writes=[r_obr[2][tile]])
                    yield from attention(lambda c0, c1: qkT[ub][:, 0, c0:c1], r_qT[ub],
                                         lambda j, h=h: kmT[:, h, j * 128:(j + 1) * 128], [r_kmT, r_kmT],
                                         lambda g: 2, False, None, None,
                                         lambda j, h=h: Vm[:, j, h * 129:(h + 1) * 129], [r_Vm, r_Vm], 129, 128 ** -0.5, fin)

            def run(gen):
                for _ in gen:
                    pass

            prev = None
            for ui in range(len(units)):
                pg = proj_gen(ui)
                if prev is None or not OPT_INTERLEAVE:
                    if prev is not None:
                        run(prev)
                    run(pg)
                else:
                    na = 88 if units[ui - 1][0] != "mem" else 12
                    nsl = NT + 3
                    for i_ in range(nsl):
                        next(pg, None)
                        for _ in range(na * (i_ + 1) // nsl - na * i_ // nsl):
                            next(prev, None)
                        next(pg, None)
                    run(prev)
                    run(pg)
                prev = attn_gen(ui) if not SKIP_ATTN else None
            if prev is not None:
                run(prev)

            if dbg and s == 0:
                dump(o_br[0][:, 0, :], r_obr[0], 512)
                dump(o_br[1][:, 0, :], r_obr[1], 512)
                dump(o_br[2][:, 0, :], r_obr[2], 512)
                dump(o_br[0][:, 15, :], r_obr[0], 512)
                dump(o_br[1][:, 15, :], r_obr[1], 512)
                dump(o_br[2][:, 15, :], r_obr[2], 512)
            k.barrier()

            if STOP_AFTER <= 1:
                continue
            RB.reset(); RC.reset()
            merged = RB.alloc([NT, D], BF16)
            r_mg = [k.res("mg%d" % t) for t in range(NT)]
            RA2 = Bump(arena, RA0 + 48 * 1024, RB0)
            Wg = RC.alloc([8, 3 * 512], BF16)
            r_Wg = [k.res("Wg%d" % i, dma=True) for i in range(3)]
            wb = RC.alloc([4, 3 * 512], BF16)
            r_wb = [k.res("wb%d" % i, dma=True) for i in range(3)]
            bg = RC.alloc([3 * D], F32)
            r_bg = k.res("bg", dma=True)
            oT = [RA2.alloc([12, 128], BF16) for _ in range(2)]
            r_oT = [k.res("oT%d" % i) for i in range(2)]
            gtmp = [RC.alloc([512], F32) for _ in range(2)]
            r_gtmp = [k.res("gtmp%d" % i) for i in range(2)]
            pr = [RC.alloc([512], F32) for _ in range(3)]
            r_pr = [k.res("pr%d" % i) for i in range(3)]
            for i in range(3):
                k.dma(AQ, bg[:, i * D:(i + 1) * D], P["b_gate"][0, i:i + 1, :].partition_broadcast(128),
                      writes=[r_bg], sem_res=r_bg)
            wbn = ["w_branch_da", "w_branch_fx", "w_branch_mem"]
            gi = [0]
            for n in range(2):
                for i in range(3):
                    wload(wb[:, :, i * 512:(i + 1) * 512], r_wb[i],
                          P[wbn[i]][0, :, n * 512:(n + 1) * 512].rearrange("(c p) n -> p c n", p=128))
                    wload(Wg[:, :, i * 512:(i + 1) * 512], r_Wg[i], win_cols(O_G + i * D + n * 512, 512))
                def emit_T(t):
                    osl = t % 2
                    for rnd in range(2):
                        pbk = rnd
                        lo, hi = (0, 8) if rnd == 0 else (8, 12)
                        for cc in range(lo, hi):
                            br, c4 = cc // 4, cc % 4
                            k.op("pe", lambda e, cc=cc, br=br, c4=c4: e.transpose(
                                psh(pbk)[:, (cc - lo) * 128:(cc - lo + 1) * 128],
                                o_br[br][:, t, c4 * 128:(c4 + 1) * 128], identb),
                                 reads=[r_obr[br][t], r_idb], writes=[PS[pbk]], signal=(cc == hi - 1), accum=True)
                        k.op("act", lambda e: e.copy(
                            oT[osl][:, lo:hi, :], psh(pbk)[:, 0:(hi - lo) * 128].rearrange("p (c t) -> p c t", c=hi - lo)),
                             reads=[PS[pbk]], writes=[r_oT[osl]])

                emit_T(0)
                for t in range(NT):
                    osl = t % 2
                    if t + 1 < NT:
                        emit_T(t + 1)
                    for i in range(3):
                        yb, gb = 2 + i, 5 + i
                        for c in range(4):
                            k.op("pe", lambda e, c=c, i=i, yb=yb, osl=osl: e.matmul(
                                psf(yb), oT[osl][:, 4 * i + c, :], wb[:, c, i * 512:(i + 1) * 512],
                                start=(c == 0), stop=(c == 3)),
                                 reads=[r_oT[osl], r_wb[i]], writes=[PS[yb]], signal=(c == 3), accum=True)
                        for c in range(8):
                            k.op("pe", lambda e, c=c, i=i, gb=gb, t=t: e.matmul(
                                psf(gb), hT[:, c, t * 128:(t + 1) * 128], Wg[:, c, i * 512:(i + 1) * 512],
                                start=(c == 0), stop=(c == 7)),
                                 reads=[hT_res[t], r_Wg[i]], writes=[PS[gb]], signal=(c == 7), accum=True)
                        gs = gi[0] % 2
                        gi[0] += 1
                        k.op("dve", lambda e, gs=gs, gb=gb, i=i, n=n: e.tensor_tensor(
                            gtmp[gs], psf(gb), bg[:, i * D + n * 512:i * D + (n + 1) * 512], ALU.add),
                             reads=[PS[gb], r_bg], writes=[r_gtmp[gs]])
                        k.op("act", lambda e, gs=gs: e.activation(gtmp[gs], gtmp[gs], AF.Sigmoid),
                             reads=[r_gtmp[gs]], writes=[r_gtmp[gs]])
                        k.op("dve", lambda e, gs=gs, yb=yb, i=i: e.tensor_tensor(pr[i], gtmp[gs], psf(yb), ALU.mult),
                             reads=[r_gtmp[gs], PS[yb]], writes=[r_pr[i]])
                    k.op("pool", lambda e: e.tensor_tensor(pr[0], pr[0], pr[1], ALU.add),
                         reads=[r_pr[0], r_pr[1]], writes=[r_pr[0]])
                    k.op("pool", lambda e, t=t, n=n: e.tensor_tensor(merged[:, t, n * 512:(n + 1) * 512], pr[0], pr[2], ALU.add),
                         reads=[r_pr[0], r_pr[2]], writes=[r_mg[t]])
            if dbg and s == 0:
                dump(merged[:, 0, :], r_mg, 1024)
            k.barrier()

            RA.reset(); RC.reset()
            x1 = RA.alloc([NT, D], F32)
            r_x1 = [k.res("x1_%d" % t, dma=True) for t in range(NT)]
            wo = RC.alloc([8, D], BF16)
            r_wo = k.res("wo", dma=True)
            xt = [RC.alloc([D], F32) for _ in range(2)]
            r_xt = [k.res("xt3_%d" % i, dma=True) for i in range(2)]
            xn = [RC.alloc([D], BF16) for _ in range(2)]
            r_xn = [k.res("xn3_%d" % i) for i in range(2)]
            sqb3 = RC.alloc([D], F32)
            r_sqb3 = k.res("sqb3")
            mT = [RC.alloc([8, 128], BF16) for _ in range(2)]
            r_mT = [k.res("mT%d" % i) for i in range(2)]
            wload(wo, r_wo, P["w_out"][0].rearrange("(c p) n -> p c n", p=128))
            for n_ in range(NT + 3):
                if n_ - 3 >= 0:
                    t = n_ - 3
                    rms_s3(xn[t % 2], r_xn[t % 2], nf_c, hT[:, :, t * 128:(t + 1) * 128], hT_res[t], 6 + t % 2)
                if 0 <= n_ - 2 < NT:
                    t = n_ - 2
                    rms_s2(t, x1[:, t, :], r_x1[t], xn[t % 2], r_xn[t % 2])
                if 0 <= n_ - 1 < NT:
                    t = n_ - 1
                    sl = t % 2
                    for n in range(2):
                        ob = 2 + 2 * sl + n
                        for c in range(8):
                            k.op("pe", lambda e, c=c: e.matmul(
                                psf(ob), mT[sl][:, c, :], wo[:, c, n * 512:(n + 1) * 512], start=(c == 0), stop=(c == 7)),
                                 reads=[r_mT[sl], r_wo], writes=[PS[ob]], signal=(c == 7), accum=True)
                        k.op("dve", lambda e: e.tensor_tensor(
                            x1[:, t, n * 512:(n + 1) * 512], psf(ob), xt[sl][:, n * 512:(n + 1) * 512], ALU.add),
                             reads=[PS[ob], r_xt[sl]], writes=[r_x1[t]])
                    rms_s1(t, x1[:, t, :], r_x1[t], sqb3, r_sqb3)
                if n_ < NT:
                    t = n_
                    sl = t % 2
                    k.dma(AQ, xt[sl], x_d[s, t * 128:(t + 1) * 128, :], writes=[r_xt[sl]], sem_res=r_xt[sl])
                    for c in range(8):
                        k.op("pe", lambda e, c=c: e.transpose(
                            psh(sl)[:, c * 128:(c + 1) * 128], merged[:, t, c * 128:(c + 1) * 128], identb),
                             reads=[r_mg[t], r_idb], writes=[PS[sl]], signal=(c == 7), accum=True)
                    k.op("act", lambda e: e.copy(mT[sl], psh(sl).rearrange("p (c t) -> p c t", c=8)),
                         reads=[PS[sl]], writes=[r_mT[sl]])
            if dbg and s == 0:
                dump(x1[:, 0, :], r_x1, 1024)
            k.barrier()

            if STOP_AFTER <= 3:
                continue
            RB.reset(); RC.reset()
            actT = RB.alloc([NFF, 512], BF16)
            r_actT = [k.res("actT%d" % p) for p in range(NFF)]
            ubuf = [RB.alloc([2, 514], F32) for _ in range(2)]
            r_ub = [[k.res("ub%d_%d" % (i, a)) for a in range(2)] for i in range(2)]
            wd = RC.alloc([NFF, D], BF16)
            r_wd = k.res("wd", dma=True)
            wup = [RC.alloc([2, 1024], BF16) for _ in range(2)]
            r_wup = [k.res("wup%d" % i, dma=True) for i in range(2)]
            uc = [[RC.alloc([512], F32) for _ in range(2)] for _ in range(2)]
            r_uc = [[k.res("uc%d_%d" % (i, a)) for a in range(2)] for i in range(2)]
            carry = RC.alloc([2 * NFF, 2], F32)
            r_carry = [k.res("carry%d" % i) for i in range(2 * NFF)]
            wload(wd, r_wd, P["w_down"][0].rearrange("(c p) n -> p c n", p=128))
            k.op("pool", lambda e: e.memset(carry, 0.0), writes=r_carry)
            cb_c, cw2_c, cw0_c, cw1_c = 24, 68, 112, 156
            wi = [0]
            pbi = [0]
            cti = [0]
            for tg in range(4):
                for p in range(NFF):
                    ws = wi[0] % 2
                    wi[0] += 1
                    us = p % 2
                    k.dma(AQ, wup[ws], wup_s[2 * p:2 * p + 2].rearrange("b p n -> p b n"),
                          reads=[r_wups[p]], writes=[r_wup[ws]], sem_res=r_wup[ws])
                    for part in range(2):
                        ch = p + NFF * part
                        pb = pbi[0] % 4
                        pbi[0] += 1
                        for c in range(8):
                            k.op("pe", lambda e, c=c: e.matmul(
                                psf(pb), wup[ws][:, part, c * 128:(c + 1) * 128], hT[:, c, tg * 512:(tg + 1) * 512],
                                start=(c == 0), stop=(c == 7)),
                                 reads=[r_wup[ws]] + hT_res[4 * tg:4 * tg + 4], writes=[PS[pb]], signal=(c == 7), accum=True)
                        ub_ = ubuf[us]
                        k.op("act", lambda e: e.copy(ub_[:, part, 0:2], carry[:, ch, :]),
                             reads=[r_carry[ch]], writes=[r_ub[us][part]])
                        k.op("act", lambda e: e.copy(ub_[:, part, 2:514], psf(pb)),
                             reads=[PS[pb]], writes=[r_ub[us][part]])
                        k.op("act", lambda e: e.copy(carry[:, ch, :], ub_[:, part, 512:514]),
                             reads=[r_ub[us][part]], writes=[r_carry[ch]])
                        k.op("act", lambda e: e.activation(uc[us][part], psf(pb), AF.Identity,
                                                           bias=pcol[:, cb_c + ch:cb_c + ch + 1],
                                                           scale=pcol[:, cw2_c + ch:cw2_c + ch + 1]),
                             reads=[PS[pb], r_pcol], writes=[r_uc[us][part]])
                        k.op("dve", lambda e: e.scalar_tensor_tensor(
                            uc[us][part], ub_[:, part, 1:513], pcol[:, cw1_c + ch:cw1_c + ch + 1], uc[us][part], ALU.mult, ALU.add),
                             reads=[r_ub[us][part], r_pcol], writes=[r_uc[us][part]])
                        k.op("dve", lambda e: e.scalar_tensor_tensor(
                            uc[us][part], ub_[:, part, 0:512], pcol[:, cw0_c + ch:cw0_c + ch + 1], uc[us][part], ALU.mult, ALU.add),
                             reads=[r_ub[us][part], r_pcol], writes=[r_uc[us][part]])
                    k.op("act", lambda e: e.activation(uc[us][0], uc[us][0], AF.Silu),
                         reads=[r_uc[us][0]], writes=[r_uc[us][0]])
                    k.op("dve", lambda e: e.tensor_tensor(actT[:, p, :], uc[us][0], uc[us][1], ALU.mult),
                         reads=[r_uc[us][0], r_uc[us][1]], writes=[r_actT[p]])
                for tt in range(4):
                    t = tg * 4 + tt
                    osl = t % 2
                    for n in range(2):
                        ob = 4 + (2 * t + n) % 4
                        for p in range(NFF):
                            k.op("pe", lambda e, p=p, n=n, ob=ob, tt=tt: e.matmul(
                                psf(ob), actT[:, p, tt * 128:(tt + 1) * 128], wd[:, p, n * 512:(n + 1) * 512],
                                start=(p == 0), stop=(p == NFF - 1)),
                                 reads=[r_actT[p], r_wd], writes=[PS[ob]], signal=(p == NFF - 1), accum=True)
                        k.op("dve", lambda e, n=n, ob=ob, t=t: e.tensor_tensor(
                            x1[:, t, n * 512:(n + 1) * 512], psf(ob), x1[:, t, n * 512:(n + 1) * 512], ALU.add),
                             reads=[PS[ob]], writes=[r_x1[t]])
                    k.dma(AQ, out_d[s, t * 128:(t + 1) * 128, :], x1[:, t, :], reads=[r_x1[t]], sem_res=r_x1[t])
            k.barrier()
        k.final()
    return nc


_NC_CACHE = {}


def kernel(**inputs):
    n = 8
    x = np.ascontiguousarray(inputs["x"], dtype=np.float32)
    mem = np.ascontiguousarray(inputs["mem"], dtype=np.float32)
    B = x.shape[0]
    per = B // n
    if "nc" not in _NC_CACHE:
        _NC_CACHE["nc"] = build(per)
    nc = _NC_CACHE["nc"]
    cst = make_consts()
    params = {nme: np.ascontiguousarray(inputs[nme], dtype=np.float32) for nme in PARAM_NAMES}
    in_maps = []
    for i in range(n):
        m = {"x": x[i * per:(i + 1) * per], "mem": mem[i * per:(i + 1) * per], "cst": cst}
        m.update(params)
        in_maps.append(m)
    res = run_bass_kernel_spmd(nc, in_maps, core_ids=list(range(n)))
    return np.concatenate([r["out"] for r in res.results], axis=0)
```

```python
import math
from contextlib import ExitStack

import numpy as np
import concourse.bass as bass
import concourse.mybir as mybir
from concourse.bass_utils import run_bass_kernel_spmd

F32 = mybir.dt.float32
BF16 = mybir.dt.bfloat16
U8 = mybir.dt.uint8
AF = mybir.ActivationFunctionType
ALU = mybir.AluOpType
AX = mybir.AxisListType

D = 1024
S = 2048
NT = S // 128
NMEM = 256
IN_COLS = 6664
DFF = 2816
NFF = DFF // 128
EPS = 1e-6
SLOPES = [2.0 ** (-8.0 * (i + 1) / 4) for i in range(4)]
LAM_INIT = 0.8 - 0.6 * math.exp(-0.3 * 0)
NEG = -30000.0
import os
OPT_INTERLEAVE = int(os.environ.get('OPT_INTERLEAVE', '1'))
MAX_INFLIGHT = int(os.environ.get('MAX_INFLIGHT', '6'))
STOP_AFTER = int(os.environ.get('STOP_AFTER', '99'))
NO_WAW_SKIP = int(os.environ.get('NO_WAW_SKIP', '0'))
UNITS_LIMIT = int(os.environ.get('UNITS_LIMIT', '12'))
SKIP_ATTN = int(os.environ.get('SKIP_ATTN', '0'))
PROJ_STAGE = int(os.environ.get('PROJ_STAGE', '9'))
DUMMY_MM = int(os.environ.get('DUMMY_MM', '0'))
O_AQ, O_AK, O_AV, O_FQ, O_FK, O_FV, O_FG, O_MQ, O_G = 0, 512, 1024, 1536, 2048, 2560, 3072, 3080, 3592

C_ID, C_U, C_ONES, C_BDA, C_BFX, C_ALK, C_DAQ, C_END = 0, 128, 256, 384, 896, 1024, 1088, 1152

PARAM_NAMES = ["norm_mix", "w_in", "b_gate", "da_q_norm", "da_k_norm", "da_lambda_q1", "da_lambda_k1",
               "da_lambda_q2", "da_lambda_k2", "da_subln", "fx_q_norm", "fx_k_norm", "fx_f_bias",
               "mem_norm", "w_mem_kv", "mem_q_norm", "mem_k_norm", "w_branch_da", "w_branch_fx",
               "w_branch_mem", "w_out", "norm_ffn", "w_up", "conv_w", "conv_b", "w_down"]


def make_consts():
    c = np.zeros((128, C_END), np.float32)
    p = np.arange(128)
    c[:, C_ID:C_ID + 128] = np.eye(128, dtype=np.float32)
    c[:, C_U:C_U + 128] = (p[:, None] <= p[None, :]).astype(np.float32)
    c[:, C_ONES:C_ONES + 128] = 1.0
    k = p[:, None]
    q = p[None, :]
    for h in range(4):
        b = np.where(k <= q, 0.0,
                     np.where((k // 64) == (q // 64), -2.0 * SLOPES[h] * (k - q), NEG))
        c[:, C_BDA + 128 * h:C_BDA + 128 * (h + 1)] = b
    c[:, C_BFX:C_BFX + 128] = np.where(k <= q, 0.0, NEG)
    for j in range(NT):
        for h in range(4):
            pos = 128 * j + p
            c[:, C_ALK + j * 4 + h] = SLOPES[h] * pos
            c[:, C_DAQ + j * 4 + h] = -SLOPES[h] * pos * 8.0
    return c


class DSem:
    def __init__(self, h):
        self.h = h
        self.count = 0


class Res:
    __slots__ = ("name", "w", "r", "dsem", "psum")

    def __init__(self, name, dsem=None):
        self.name = name
        self.psum = False
        self.w = None
        self.r = {}
        self.dsem = dsem


class K:
    def __init__(self, nc, es):
        self.nc = nc
        self.es = es
        self.eng = {"pe": nc.tensor, "act": nc.scalar, "dve": nc.vector, "pool": nc.gpsimd, "sp": nc.sync}
        self.sem = {}
        self.cnt = {}
        self.seen = {}
        for e in self.eng:
            self.sem[e] = es.enter_context(nc.semaphore("sem_" + e))
            self.cnt[e] = 0
            self.seen[e] = {}
        self.dsems = []
        self.nres = 0
        self.inflight = {}

    def res(self, name, dma=False):
        self.nres += 1
        ds = None
        if dma:
            ds = DSem(self.es.enter_context(self.nc.semaphore("d_%s_%d" % (name, self.nres))))
            self.dsems.append(ds)
        return Res(name, ds)

    def wait(self, e, tok):
        if tok is None:
            return
        sem, val = tok
        key = sem.num
        if self.seen[e].get(key, 0) >= val:
            return
        self.eng[e].wait_ge(sem, val)
        self.seen[e][key] = val

    def _deps(self, e, reads, writes, accum):
        for r in reads:
            self.wait(e, r.w)
            if r.psum:
                for t in list(r.r.values()):
                    if t[0] is not self.sem[e]:
                        self.wait(e, t)
        for w in writes:
            if not (accum and w.w is not None and w.w[0] is self.sem[e]):
                self.wait(e, w.w)
            for t in list(w.r.values()):
                self.wait(e, t)

    def _record(self, tok, reads, writes):
        for r in reads:
            old = r.r.get(tok[0].num)
            if old is None or old[1] < tok[1]:
                r.r[tok[0].num] = tok
        for w in writes:
            w.w = tok
            w.r = {}

    def op(self, e, fn, reads=(), writes=(), signal=True, accum=False):
        self._deps(e, reads, writes, accum)
        ins = fn(self.eng[e])
        if signal:
            self.cnt[e] += 1
            ins.then_inc(self.sem[e], 1)
            tok = (self.sem[e], self.cnt[e])
        else:
            tok = (self.sem[e], self.cnt[e] + 1)
        self._record(tok, reads, writes)
        return ins

    def dma(self, q, out, in_, reads=(), writes=(), sem_res=None, **kw):
        ds = sem_res.dsem
        for r in reads:
            self.wait(q, r.w)
        for w in writes:
            if NO_WAW_SKIP or not (w.w is not None and w.w[0] is ds.h):
                self.wait(q, w.w)
            for t in list(w.r.values()):
                self.wait(q, t)
        fl = self.inflight.setdefault(q, [])
        if len(fl) >= MAX_INFLIGHT:
            ds_old = fl.pop(0)
            self.wait(q, (ds_old.h, ds_old.count))
        ins = self.eng[q].dma_start(out=out, in_=in_, **kw)
        ds.count += 16
        ins.then_inc(ds.h, 16)
        tok = (ds.h, ds.count)
        fl.append(ds)
        self._record(tok, reads, writes)
        return ins

    def barrier(self):
        toks = [(self.sem[e], self.cnt[e]) for e in self.eng if self.cnt[e] > 0]
        toks += [(d.h, d.count) for d in self.dsems if d.count > 0]
        for e in self.eng:
            for t in toks:
                self.wait(e, t)

    def final(self):
        toks = [(d.h, d.count) for d in self.dsems if d.count > 0]
        toks += [(self.sem[e], self.cnt[e]) for e in self.eng if self.cnt[e] > 0]
        for t in toks:
            self.wait("sp", t)


class Bump:
    def __init__(self, arena, lo, hi):
        self.arena, self.lo, self.hi, self.cur = arena, lo, hi, lo

    def reset(self):
        self.cur = self.lo

    def alloc(self, shape, dt):
        n = int(np.prod(shape))
        sz = n * (4 if dt == F32 else 2)
        off = (self.cur + 31) // 32 * 32
        assert off + sz <= self.hi, ("arena overflow", off, sz, self.hi)
        self.cur = off + sz
        ap = self.arena[:, off:off + sz].bitcast(dt)
        if len(shape) == 2:
            ap = ap.rearrange("p (a b) -> p a b", a=shape[0])
        elif len(shape) == 3:
            ap = ap.rearrange("p (a b c) -> p a b c", a=shape[0], b=shape[1])
        return ap


def build(nseq=2, dbg=False):
    nc = bass.Bass("TRN2", target_bir_lowering=False)
    x_d = nc.dram_tensor("x", [nseq, S, D], F32, kind="ExternalInput").ap()
    mem_d = nc.dram_tensor("mem", [nseq, NMEM, D], F32, kind="ExternalInput").ap()
    cst_d = nc.dram_tensor("cst", [128, C_END], F32, kind="ExternalInput").ap()
    shp = {"norm_mix": [1, D], "w_in": [1, D, IN_COLS], "b_gate": [1, 3, D], "da_q_norm": [1, 64],
           "da_k_norm": [1, 64], "da_lambda_q1": [1, 64], "da_lambda_k1": [1, 64], "da_lambda_q2": [1, 64],
           "da_lambda_k2": [1, 64], "da_subln": [1, 128], "fx_q_norm": [1, 64], "fx_k_norm": [1, 64],
           "fx_f_bias": [1, 8], "mem_norm": [1, D], "w_mem_kv": [1, D, 1024], "mem_q_norm": [1, 128],
           "mem_k_norm": [1, 128], "w_branch_da": [1, 512, D], "w_branch_fx": [1, 512, D],
           "w_branch_mem": [1, 512, D], "w_out": [1, D, D], "norm_ffn": [1, D], "w_up": [1, D, 2 * DFF],
           "conv_w": [1, 3, 2 * DFF], "conv_b": [1, 2 * DFF], "w_down": [1, DFF, D]}
    P = {n: nc.dram_tensor(n, shp[n], F32, kind="ExternalInput").ap() for n in PARAM_NAMES}
    out_d = nc.dram_tensor("out", [nseq, S, D], F32, kind="ExternalOutput").ap()
    wup_s = nc.dram_tensor("wup_s", [2 * NFF, 128, 1024], BF16).ap()
    dbg_d = nc.dram_tensor("dbg", [128, 8192], F32, kind="ExternalOutput").ap() if dbg else None

    WQ = "pool"
    AQ = "sp"

    with ExitStack() as es:
        ARENA = 207 * 1024
        arena = es.enter_context(nc.sbuf_tensor("arena", [128, ARENA], U8))
        psb = [es.enter_context(nc.psum_tensor("ps%d" % i, [128, 512], F32)) for i in range(8)]
        k = K(nc, es)
        PS = [k.res("ps%d" % i) for i in range(8)]
        for r_ in PS:
            r_.psum = True

        def psf(i):
            return psb[i][:, :]

        def psh(i):
            return psb[i][:, :].bitcast(BF16)

        CB = Bump(arena, 0, 16 * 1024)
        HT0 = 16 * 1024
        RA0 = HT0 + 32 * 1024
        RB0 = RA0 + 64 * 1024
        RC0 = RB0 + 32 * 1024
        hT = Bump(arena, HT0, RA0).alloc([8, S], BF16)
        RA = Bump(arena, RA0, RB0)
        RB = Bump(arena, RB0, RC0)
        RC = Bump(arena, RC0, ARENA)
        hT_res = [k.res("hT%d" % t) for t in range(NT)]
        r_wups = [k.res("wups%d" % p, dma=True) for p in range(NFF)]

        cst = CB.alloc([C_END], F32)
        r_cst = k.res("cst", dma=True)
        k.dma(AQ, cst, cst_d, writes=[r_cst], sem_res=r_cst)
        identb = CB.alloc([128], BF16)
        r_idb = k.res("identb", dma=True)
        k.dma(WQ, identb, cst_d[:, C_ID:C_ID + 128], writes=[r_idb], sem_res=r_idb)
        identf = cst[:, C_ID:C_ID + 128]
        Umat = cst[:, C_U:C_U + 128]
        ones_f = cst[:, C_ONES:C_ONES + 128]

        r_par = k.res("params", dma=True)

        def bload(name, n, reps=1):
            if isinstance(name, str):
                name = [name] * reps
            t = CB.alloc([n * len(name)], F32)
            for r, nm in enumerate(name):
                k.dma(AQ, t[:, r * n:(r + 1) * n], P[nm][0:1, :].partition_broadcast(128),
                      writes=[r_par], sem_res=r_par)
            return t

        g_da4 = bload(["da_q_norm", "da_q_norm", "da_k_norm", "da_k_norm"], 64)
        g_fx4 = bload(["fx_q_norm", "fx_q_norm", "fx_k_norm", "fx_k_norm"], 64)
        g_mq = bload("mem_q_norm", 128)
        g_mk4 = bload("mem_k_norm", 128, 4)
        g_sub = bload("da_subln", 128)
        fbias = bload("fx_f_bias", 8)
        lq1 = bload("da_lambda_q1", 64)
        lk1 = bload("da_lambda_k1", 64)
        lq2 = bload("da_lambda_q2", 64)
        lk2 = bload("da_lambda_k2", 64)

        rows1 = CB.alloc([128], F32)
        rows2 = CB.alloc([128], F32)
        r_rows = k.res("rows", dma=True)
        k.dma(AQ, rows1[0:8, :], P["norm_mix"][0].rearrange("(c p) -> c p", p=128), writes=[r_rows], sem_res=r_rows)
        k.dma(AQ, rows1[8:16, :], P["norm_ffn"][0].rearrange("(c p) -> c p", p=128), writes=[r_rows], sem_res=r_rows)
        k.dma(AQ, rows1[16:24, :], P["mem_norm"][0].rearrange("(c p) -> c p", p=128), writes=[r_rows], sem_res=r_rows)
        k.dma(AQ, rows1[24:68, :], P["conv_b"][0].rearrange("(c p) -> c p", p=128), writes=[r_rows], sem_res=r_rows)
        k.dma(AQ, rows1[68:112, :], P["conv_w"][0, 2].rearrange("(c p) -> c p", p=128), writes=[r_rows], sem_res=r_rows)
        k.dma(AQ, rows2[0:44, :], P["conv_w"][0, 0].rearrange("(c p) -> c p", p=128), writes=[r_rows], sem_res=r_rows)
        k.dma(AQ, rows2[44:88, :], P["conv_w"][0, 1].rearrange("(c p) -> c p", p=128), writes=[r_rows], sem_res=r_rows)
        pcol = CB.alloc([200], F32)
        r_pcol = k.res("pcol")
        k.op("pe", lambda e: e.matmul(psf(0)[:, 0:112], rows1[0:112, :], identf[0:112, 0:112], start=True, stop=True),
             reads=[r_rows, r_cst], writes=[PS[0]])
        k.op("pe", lambda e: e.matmul(psf(0)[:, 112:200], rows2[0:88, :], identf[0:88, 0:88], start=True, stop=True),
             reads=[r_rows, r_cst], writes=[PS[0]], accum=True)
        k.op("dve", lambda e: e.tensor_copy(pcol, psf(0)[:, 0:200]), reads=[PS[0]], writes=[r_pcol])
        nm_c, nf_c, mn_c = pcol[:, 0:8], pcol[:, 8:16], pcol[:, 16:24]

        lam_t = CB.alloc([8], F32)
        r_lam = k.res("lam")
        junk = CB.alloc([64], F32)
        r_junk = k.res("junk")
        k.op("dve", lambda e: e.tensor_tensor(junk, lq1, lk1, ALU.mult), reads=[r_par], writes=[r_junk])
        k.op("dve", lambda e: e.tensor_reduce(lam_t[:, 0:1], junk, AX.X, ALU.add), reads=[r_junk], writes=[r_lam])
        k.op("dve", lambda e: e.tensor_tensor(junk, lq2, lk2, ALU.mult), reads=[r_par, r_lam], writes=[r_junk])
        k.op("dve", lambda e: e.tensor_reduce(lam_t[:, 1:2], junk, AX.X, ALU.add), reads=[r_junk], writes=[r_lam])
        k.op("act", lambda e: e.activation(lam_t[:, 2:4], lam_t[:, 0:2], AF.Exp), reads=[r_lam], writes=[r_lam])
        k.op("dve", lambda e: e.tensor_tensor(lam_t[:, 4:5], lam_t[:, 2:3], lam_t[:, 3:4], ALU.subtract),
             reads=[r_lam], writes=[r_lam])
        k.op("dve", lambda e: e.tensor_scalar(lam_t[:, 5:6], lam_t[:, 4:5], -1.0, -LAM_INIT, ALU.mult, ALU.add),
             reads=[r_lam], writes=[r_lam])
        neg_lam = lam_t[:, 5:6]
        k.op("dve", lambda e: e.tensor_scalar(g_sub, g_sub, 1.0 - LAM_INIT, None, ALU.mult),
             reads=[r_par], writes=[r_par])
        zall = CB.alloc([NT, 8], F32)
        logf = CB.alloc([NT, 8], F32)
        negc = CB.alloc([NT, 8], F32)
        c8 = CB.alloc([NT, 8], F32)
        r_z, r_logf, r_negc, r_c8 = k.res("z"), k.res("logf"), k.res("negc"), k.res("c8")
        pref = zall
        r_pref = r_z
        small = CB.alloc([128], F32)
        r_small = [k.res("small%d" % i) for i in range(16)]
        small_i = [0]

        def get_small():
            i = small_i[0] % 16
            small_i[0] += 1
            return small[:, i * 8:(i + 1) * 8], r_small[i]

        def rstd_from_ss(ss_ap, r_ss, n, inv_d):
            k.op("act", lambda e: e.activation(ss_ap, ss_ap, AF.Ln, bias=eps_c, scale=inv_d),
                 reads=[r_ss, r_par], writes=[r_ss])
            k.op("act", lambda e: e.activation(ss_ap, ss_ap, AF.Exp, scale=-0.5), reads=[r_ss], writes=[r_ss])

        eps_c = CB.alloc([1], F32)
        k.op("pool", lambda e: e.memset(eps_c, EPS), writes=[r_par])

        rss = CB.alloc([8], F32)
        r_rss = [k.res("rss%d" % i) for i in range(4)]

        def rms_s1(t, src_ap, r_src, sq_ap, r_sq):
            ss = rss[:, t % 4:t % 4 + 1]
            k.op("act", lambda e: e.activation(sq_ap, src_ap, AF.Square, accum_out=ss),
                 reads=[r_src], writes=[r_sq, r_rss[t % 4]])

        def rms_s2(t, src_ap, r_src, xn_ap, r_xn):
            ss = rss[:, t % 4:t % 4 + 1]
            rstd_from_ss(ss, r_rss[t % 4], 1, 1.0 / D)
            k.op("act", lambda e: e.activation(xn_ap, src_ap, AF.Copy, scale=ss),
                 reads=[r_src, r_rss[t % 4]], writes=[r_xn])

        def rms_s3(xn_ap, r_xn, gain_cols, dstT, r_dst, pbank):
            for c in range(8):
                k.op("pe", lambda e, c=c: e.transpose(psh(pbank)[:, c * 128:(c + 1) * 128],
                                                      xn_ap[:, c * 128:(c + 1) * 128], identb),
                     reads=[r_xn, r_idb], writes=[PS[pbank]], signal=(c == 7), accum=True)
            k.op("dve", lambda e: e.tensor_tensor(dstT, psh(pbank).rearrange("p (c t) -> p c t", c=8),
                                                  gain_cols.unsqueeze(2).broadcast_to([128, 8, 128]), ALU.mult),
                 reads=[PS[pbank], r_pcol], writes=[r_dst])

        def rms_to_T(src_ap, r_src, xn_ap, r_xn, sq_ap, r_sq, gain_cols, dstT, r_dst, pbank, t=0):
            rms_s1(t, src_ap, r_src, sq_ap, r_sq)
            rms_s2(t, src_ap, r_src, xn_ap, r_xn)
            rms_s3(xn_ap, r_xn, gain_cols, dstT, r_dst, pbank)

        def qknorm(ps_ap, r_ps, H, d, gain_ap, out_ap, r_out, sq_ap, r_sq, tmp_ap, r_tmp):
            ss, r_ss = get_small()
            k.op("act", lambda e: e.activation(sq_ap[:, 0:H * d], ps_ap, AF.Square), reads=[r_ps], writes=[r_sq])
            k.op("dve", lambda e: e.tensor_reduce(ss[:, 0:H], sq_ap[:, 0:H * d].rearrange("p (h d) -> p h d", h=H),
                                                  AX.X, ALU.add), reads=[r_sq], writes=[r_ss])
            rstd_from_ss(ss[:, 0:H], r_ss, H, 1.0 / d)
            k.op("dve", lambda e: e.tensor_tensor(tmp_ap[:, 0:H * d].rearrange("p (h d) -> p h d", h=H),
                                                  ps_ap.rearrange("p (h d) -> p h d", h=H),
                                                  ss[:, 0:H].unsqueeze(2).broadcast_to([128, H, d]), ALU.mult),
                 reads=[r_ps, r_ss], writes=[r_tmp])
            k.op("dve", lambda e: e.tensor_tensor(out_ap, tmp_ap[:, 0:H * d].rearrange("p (h d) -> p h d", h=H),
                                                  gain_ap.rearrange("p (h d) -> p h d", h=H), ALU.mult),
                 reads=[r_tmp, r_par], writes=[r_out])

        def wload(dst, r_dst, src, eng=None):
            k.dma(eng or WQ, dst, src, writes=[r_dst], sem_res=r_dst)

        def win_cols(c0, n):
            return P["w_in"][0, :, c0:c0 + n].rearrange("(c p) n -> p c n", p=128)

        dbg_n = [0]

        def dump(ap, reads, ncols):
            if not dbg:
                return
            r = k.res("dbg", dma=True)
            k.dma("pool", dbg_d[:, dbg_n[0]:dbg_n[0] + ncols], ap, reads=reads, writes=[r], sem_res=r)
            dbg_n[0] += ncols

        for s in range(nseq):
            RA.reset(); RB.reset(); RC.reset()
            o_br = [RA.alloc([NT, 512], BF16) for _ in range(3)]
            r_obr = [[k.res("o%d_%d" % (b, t)) for t in range(NT)] for b in range(3)]
            xt = [RA.alloc([D], F32) for _ in range(2)]
            r_xt = [k.res("xt%d" % i, dma=True) for i in range(2)]
            xn = [RA.alloc([D], BF16) for _ in range(2)]
            r_xn = [k.res("xn%d" % i) for i in range(2)]
            wkv = Bump(arena, RC0 + 13 * 1024, ARENA).alloc([8, 1024], BF16)
            r_wkv = k.res("wkv", dma=True)
            mhT = RB.alloc([8, NMEM], BF16)
            r_mhT = [k.res("mhT%d" % i) for i in range(2)]
            kmT = RB.alloc([4, NMEM], BF16)
            r_kmT = k.res("kmT")
            Vm = RB.alloc([2, 4 * 129], BF16)
            r_Vm = k.res("Vm")
            kbm = RB.alloc([4, 128], BF16)
            r_kbm = k.res("kbm")
            wu = [RC.alloc([8, 384], BF16) for _ in range(2)]
            r_wu = [k.res("wu%d" % i, dma=True) for i in range(2)]
            qkT = [RC.alloc([4, S], BF16) for _ in range(2)]
            Va = [RC.alloc([NT, 130], BF16) for _ in range(2)]
            r_qT = [[k.res("qkT%d_%d" % (i, t)) for t in range(NT)] for i in range(2)]
            r_Va = [[k.res("Va%d_%d" % (i, t)) for t in range(NT)] for i in range(2)]
            qka = [RA.alloc([4, 65], BF16) for _ in range(2)]
            r_qka = [k.res("qka%d" % i) for i in range(2)]
            qmem = [RA.alloc([128], BF16) for _ in range(2)]
            praw = [RA.alloc([256], F32) for _ in range(2)]
            r_praw = [k.res("praw%d" % i) for i in range(2)]
            sqj = RB.alloc([D], F32)
            r_sq512 = k.res("sqj")
            r_tmp512 = r_sq512
            sq512 = sqj[:, 0:512]
            tmp512 = sqj[:, 512:1024]
            sqb = [sqj]
            r_sqb = [r_sq512]
            sqs = [RB.alloc([256], F32) for _ in range(2)]
            r_sqs = [k.res("sqs%d" % i) for i in range(2)]
            tmps = [RB.alloc([256], F32) for _ in range(2)]
            r_tmps = [k.res("tmps%d" % i) for i in range(2)]
            PT = [RB.alloc([512], BF16) for _ in range(4)]
            r_PT = [k.res("PT%d" % i) for i in range(4)]
            tdiag = [RB.alloc([128], F32) for _ in range(2)]
            r_tdiag = [k.res("tdiag%d" % i) for i in range(2)]
            O0n = RB.alloc([NT, 128], F32)
            r_O0n = [k.res("O0n%d" % i) for i in range(NT)]
            finjunk = RB.alloc([128], F32)
            pss = RB.alloc([24], F32)
            r_pss = [k.res("pss%d" % i) for i in range(3)]
            fss = RB.alloc([16], F32)
            r_fss = [k.res("fss%d" % i) for i in range(16)]
            r_finjunk = k.res("finjunk")
            wfg = RC.alloc([8, 8], BF16)
            r_wfg = k.res("wfg", dma=True)

            wload(wkv, r_wkv, P["w_mem_kv"][0].rearrange("(c p) n -> p c n", p=128))
            wload(wfg, r_wfg, win_cols(O_FG, 8))
            k.op("pool", lambda e: e.memset(Vm.rearrange("p a (h d) -> p a h d", h=4)[:, :, :, 128:129], 1.0),
                 writes=[r_Vm])
            for i_ in range(2):
                k.op("pool", lambda e, i_=i_: e.memset(qka[i_][:, 2:4, 64:65], 1.0), writes=[r_qka[i_]])
            for mt in range(2):
                sl = mt % 2
                k.dma(AQ, xt[sl], mem_d[s, mt * 128:(mt + 1) * 128, :], writes=[r_xt[sl]], sem_res=r_xt[sl])
                rms_to_T(xt[sl], r_xt[sl], xn[sl], r_xn[sl], sqb[0], r_sqb[0], mn_c,
                         mhT[:, :, mt * 128:(mt + 1) * 128], r_mhT[mt], 7)
            for mt in range(2):
                for half in range(2):
                    pb = 5 + half
                    for c in range(8):
                        k.op("pe", lambda e, c=c, pb=pb, half=half, mt=mt: e.matmul(
                            psf(pb), mhT[:, c, mt * 128:(mt + 1) * 128], wkv[:, c, half * 512:(half + 1) * 512],
                            start=(c == 0), stop=(c == 7)),
                             reads=[r_mhT[mt], r_wkv], writes=[PS[pb]], signal=(c == 7), accum=True)
                    if half == 0:
                        qknorm(psf(pb), PS[pb], 4, 128, g_mk4, kbm, r_kbm, sq512, r_sq512, tmp512, r_tmp512)
                        for h in range(4):
                            k.op("pe", lambda e, h=h: e.transpose(psh(7)[:, h * 128:(h + 1) * 128], kbm[:, h, :], identb),
                                 reads=[r_kbm, r_idb], writes=[PS[7]], signal=(h == 3), accum=True)
                        k.op("act", lambda e, mt=mt: e.copy(kmT[:, :, mt * 128:(mt + 1) * 128],
                                                            psh(7)[:, 0:512].rearrange("p (h t) -> p h t", h=4)),
                             reads=[PS[7]], writes=[r_kmT])
                    else:
                        k.op("act", lambda e, mt=mt, pb=pb: e.copy(
                            Vm[:, mt, :].rearrange("p (h d) -> p h d", h=4)[:, :, 0:128],
                            psf(pb).rearrange("p (h d) -> p h d", h=4)), reads=[PS[pb]], writes=[r_Vm])

            for n_ in range(NT + 2):
                if n_ - 2 >= 0:
                    t = n_ - 2
                    rms_s3(xn[t % 2], r_xn[t % 2], nm_c, hT[:, :, t * 128:(t + 1) * 128], hT_res[t], 6 + t % 2)
                if 0 <= n_ - 1 < NT:
                    t = n_ - 1
                    rms_s2(t, xt[t % 2], r_xt[t % 2], xn[t % 2], r_xn[t % 2])
                if n_ < NT:
                    t = n_
                    k.dma(AQ, xt[t % 2], x_d[s, t * 128:(t + 1) * 128, :], writes=[r_xt[t % 2]], sem_res=r_xt[t % 2])
                    rms_s1(t, xt[t % 2], r_xt[t % 2], sqb[0], r_sqb[0])

            for t in range(NT):
                for c in range(8):
                    k.op("pe", lambda e, c=c, t=t: e.matmul(psf(6)[:, t * 8:(t + 1) * 8], hT[:, c, t * 128:(t + 1) * 128],
                                                           wfg[:, c, :], start=(c == 0), stop=(c == 7)),
                         reads=[hT_res[t], r_wfg], writes=[PS[6]], signal=(c == 7 and t == NT - 1), accum=True)
            k.op("dve", lambda e: e.tensor_tensor(zall, psf(6)[:, 0:NT * 8].rearrange("p (t h) -> p t h", t=NT),
                                                  fbias.unsqueeze(1).broadcast_to([128, NT, 8]), ALU.add),
                 reads=[PS[6], r_par], writes=[r_z])
            k.op("act", lambda e: e.activation(zall, zall, AF.Exp, scale=-1.0), reads=[r_z], writes=[r_z])
            k.op("act", lambda e: e.activation(zall, zall, AF.Ln, bias=1.0), reads=[r_z], writes=[r_z])
            k.op("dve", lambda e: e.tensor_scalar(logf, zall, -1.0, None, ALU.mult), reads=[r_z], writes=[r_logf])
            k.op("pe", lambda e: e.matmul(psf(5)[:, 0:NT * 8], ones_f, logf.rearrange("p t h -> p (t h)"),
                                          start=True, stop=True), reads=[r_logf, r_cst], writes=[PS[5]])
            k.op("dve", lambda e: e.memset(pref[:, 0, :], 0.0), writes=[r_pref])
            for t in range(1, NT):
                k.op("dve", lambda e, t=t: e.tensor_tensor(pref[:, t, :], pref[:, t - 1, :], psf(5)[:, (t - 1) * 8:t * 8], ALU.add),
                     reads=[PS[5], r_pref], writes=[r_pref])
            for t in range(NT):
                k.op("pe", lambda e, t=t: e.matmul(psf(6)[:, t * 8:(t + 1) * 8], Umat, logf[:, t, :], start=True, stop=True),
                     reads=[r_logf, r_cst], writes=[PS[6]], signal=(t == NT - 1), accum=True)
            k.op("dve", lambda e: e.tensor_tensor(pref, pref, psf(6)[:, 0:NT * 8].rearrange("p (t h) -> p t h", t=NT), ALU.add),
                 reads=[PS[6], r_pref], writes=[r_pref])
            k.op("dve", lambda e: e.tensor_scalar(negc, pref, -1.0, None, ALU.mult), reads=[r_pref], writes=[r_negc])
            k.op("dve", lambda e: e.tensor_scalar(c8, pref, 8.0, None, ALU.mult), reads=[r_pref], writes=[r_c8])

            k.barrier()

            pt_i = [0]
            st_i = [0]
            td_i = [0]

            def attention(qT_of, r_q, kT_of, r_k, nkt_of_g, causal, bias_of, Bmat, V_of, r_V, dvp1, scale, finish):
                for g in range(4):
                    nk = nkt_of_g(g)
                    pend = None
                    for j in range(nk + 1):
                        if j < nk:
                            col0 = 128 * (j - 4 * g) if (causal and j >= 4 * g) else 0
                            sb = 4 + (st_i[0] % 2)
                            st_i[0] += 1
                            q_r = [r_q[t] for t in range(4 * g + col0 // 128, 4 * g + 4)]
                            k.op("pe", lambda e, sb=sb, j=j, col0=col0, g=g: e.matmul(
                                psf(sb)[:, col0:512], kT_of(j), qT_of(512 * g + col0, 512 * (g + 1)),
                                start=True, stop=True), reads=q_r + [r_k[j]], writes=[PS[sb]])
                            pi = pt_i[0] % 4
                            pt_i[0] += 1
                            c1 = col0
                            if causal and j >= 4 * g:
                                di = td_i[0] % 2
                                td_i[0] += 1
                                k.op("dve", lambda e, sb=sb, col0=col0, di=di: e.scalar_tensor_tensor(
                                    tdiag[di], psf(sb)[:, col0:col0 + 128], scale, Bmat, ALU.mult, ALU.add),
                                     reads=[PS[sb], r_cst], writes=[r_tdiag[di]])
                                k.op("act", lambda e, di=di, pi=pi, col0=col0, j=j: e.activation(
                                    PT[pi][:, col0:col0 + 128], tdiag[di], AF.Exp, bias=bias_of(j), scale=1.0),
                                     reads=[r_tdiag[di], r_negc, r_cst], writes=[r_PT[pi]])
                                c1 = col0 + 128
                            if c1 < 512:
                                if bias_of is not None:
                                    k.op("act", lambda e, sb=sb, pi=pi, c1=c1, j=j: e.activation(
                                        PT[pi][:, c1:512], psf(sb)[:, c1:512], AF.Exp, bias=bias_of(j), scale=scale),
                                         reads=[PS[sb], r_negc, r_cst], writes=[r_PT[pi]])
                                else:
                                    k.op("act", lambda e, sb=sb, pi=pi, c1=c1: e.activation(
                                        PT[pi][:, c1:512], psf(sb)[:, c1:512], AF.Exp, scale=scale),
                                         reads=[PS[sb]], writes=[r_PT[pi]])
                            cur = (j, pi, col0)
                        else:
                            cur = None
                        if pend is not None:
                            jj, pi2, col02 = pend
                            had_fin = False
                            for i in range(col02 // 128, 4):
                                last = (jj == 4 * g + i) if causal else (jj == nk - 1)
                                k.op("pe", lambda e, i=i, pi2=pi2, jj=jj, last=last: e.matmul(
                                    psf(i)[:, 0:dvp1], PT[pi2][:, i * 128:(i + 1) * 128], V_of(jj),
                                    start=(jj == 0), stop=last),
                                     reads=[r_PT[pi2], r_V[jj]], writes=[PS[i]], accum=(jj != 0), signal=(i == 3))
                                if last:
                                    finish(4 * g + i, i)
                                    had_fin = True
                            if DUMMY_MM and not had_fin:
                                for b_ in range(4 - DUMMY_MM, 4):
                                    k.op("pe", lambda e, b_=b_: e.matmul(
                                        psf(b_)[:, 256:512], identb, PT[pi2][:, 0:256], start=False, stop=False,
                                        skip_group_check=True),
                                         reads=[r_PT[pi2], r_idb], writes=[PS[b_]], signal=False, accum=True)
                        pend = cur
                        tick()
                        yield

            units = [("da", h) for h in range(4)] + [("fx", p) for p in range(4)] + [("mem", h) for h in range(4)]
            units = units[:UNITS_LIMIT]

            def proj_gen(ui):
                kind, idx = units[ui]
                ub = ui % 2
                wsl = ui % 2
                if kind == "da":
                    offs = (O_AQ, O_AK, O_AV)
                elif kind == "fx":
                    offs = (O_FQ, O_FK, O_FV)
                else:
                    offs = (O_MQ,)
                for oi_, o_ in enumerate(offs):
                    wload(wu[wsl][:, :, oi_ * 128:(oi_ + 1) * 128], r_wu[wsl], win_cols(o_ + 128 * idx, 128))
                ncol = 128 * len(offs)
                if s == 0:
                    for p_ in range(2 * ui, min(2 * ui + 2, NFF)):
                        for part_ in range(2):
                            k.dma(WQ, wup_s[2 * p_ + part_].rearrange("p (c n) -> p c n", c=8),
                                  P["w_up"][0, :, part_ * DFF + 128 * p_:part_ * DFF + 128 * (p_ + 1)].rearrange(
                                      "(c p) n -> p c n", p=128), writes=[r_wups[p_]], sem_res=r_wups[p_])
                if kind == "da":
                    k.op("pool", lambda e: e.memset(Va[ub][:, :, 128:129], 1.0), writes=r_Va[ub])
                elif kind == "fx":
                    k.op("pool", lambda e: e.memset(
                        Va[ub].rearrange("p t (h d) -> p t h d", h=2)[:, :, :, 64:65], 1.0), writes=r_Va[ub])

                def tail(t):
                    qs = t % 2
                    if kind in ("da", "fx"):
                        for m in range(4):
                            k.op("pe", lambda e, m=m: e.transpose(psh(7)[0:65, m * 128:(m + 1) * 128],
                                                                   qka[qs][:, m, :], identb),
                                 reads=[r_qka[qs], r_idb], writes=[PS[7]], signal=(m == 3), accum=True)
                        k.op("dve", lambda e: e.tensor_copy(qkT[ub][0:65, :, t * 128:(t + 1) * 128],
                                                            psh(7)[0:65, 0:512].rearrange("p (m t) -> p m t", m=4)),
                             reads=[PS[7]], writes=[r_qT[ub][t]])
                    else:
                        k.op("pe", lambda e: e.transpose(psh(7)[:, 0:128], qmem[qs], identb),
                             reads=[r_qka[qs], r_idb], writes=[PS[7]])
                        k.op("act", lambda e: e.copy(qkT[ub][:, 0, t * 128:(t + 1) * 128], psh(7)[:, 0:128]),
                             reads=[PS[7]], writes=[r_qT[ub][t]])

                nqk = 256 if kind != "mem" else 128
                H, dd = (4, 64) if kind != "mem" else (1, 128)

                def stepA(t):
                    pb = 6
                    qs = t % 2
                    for c in range(8):
                        k.op("pe", lambda e, c=c: e.matmul(
                            psf(pb)[:, 0:ncol], hT[:, c, t * 128:(t + 1) * 128], wu[wsl][:, c, 0:ncol],
                            start=(c == 0), stop=(c == 7)),
                             reads=[hT_res[t], r_wu[wsl]], writes=[PS[pb]], signal=(c == 7), accum=True)
                    k.op("dve", lambda e: e.tensor_copy(praw[qs][:, 0:nqk], psf(pb)[:, 0:nqk]),
                         reads=[PS[pb]], writes=[r_praw[qs]])
                    if kind == "da":
                        k.op("act", lambda e: e.copy(Va[ub][:, t, 0:128], psf(pb)[:, 256:384]),
                             reads=[PS[pb]], writes=[r_Va[ub][t]])
                    elif kind == "fx":
                        k.op("act", lambda e: e.copy(
                            Va[ub][:, t, :].rearrange("p (h d) -> p h d", h=2)[:, :, 0:64],
                            psf(pb)[:, 256:384].rearrange("p (h d) -> p h d", h=2)),
                             reads=[PS[pb]], writes=[r_Va[ub][t]])
                    ss = pss[:, (t % 3) * 8:(t % 3) * 8 + 8]
                    r_ss = r_pss[t % 3]
                    k.op("act", lambda e: e.activation(sqs[qs][:, 0:nqk], praw[qs][:, 0:nqk], AF.Square),
                         reads=[r_praw[qs]], writes=[r_sqs[qs]])
                    k.op("dve", lambda e: e.tensor_reduce(ss[:, 0:H], sqs[qs][:, 0:nqk].rearrange("p (h d) -> p h d", h=H),
                                                          AX.X, ALU.add), reads=[r_sqs[qs]], writes=[r_ss])

                def stepB(t):
                    qs = t % 2
                    ss = pss[:, (t % 3) * 8:(t % 3) * 8 + 8]
                    r_ss = r_pss[t % 3]
                    rstd_from_ss(ss[:, 0:H], r_ss, H, 1.0 / dd)
                    k.op("dve", lambda e: e.tensor_tensor(
                        tmps[qs][:, 0:nqk].rearrange("p (h d) -> p h d", h=H),
                        praw[qs][:, 0:nqk].rearrange("p (h d) -> p h d", h=H),
                        ss[:, 0:H].unsqueeze(2).broadcast_to([128, H, dd]), ALU.mult),
                         reads=[r_praw[qs], r_ss], writes=[r_tmps[qs]])
                    if kind != "mem":
                        gt = g_da4 if kind == "da" else g_fx4
                        k.op("dve", lambda e: e.tensor_tensor(
                            qka[qs][:, :, 0:64], tmps[qs][:, 0:256].rearrange("p (h d) -> p h d", h=4),
                            gt.rearrange("p (h d) -> p h d", h=4), ALU.mult),
                             reads=[r_tmps[qs], r_par], writes=[r_qka[qs]])
                        if kind == "da":
                            k.op("dve", lambda e: e.tensor_copy(
                                qka[qs][:, 0:2, 64:65],
                                cst[:, C_DAQ + t * 4 + idx:C_DAQ + t * 4 + idx + 1].unsqueeze(1).broadcast_to([128, 2, 1])),
                                 reads=[r_cst], writes=[r_qka[qs]])
                        else:
                            k.op("dve", lambda e: e.tensor_copy(
                                qka[qs][:, 0:2, 64:65], c8[:, t, 2 * idx:2 * idx + 2].unsqueeze(2)),
                                 reads=[r_c8], writes=[r_qka[qs]])
                    else:
                        k.op("dve", lambda e: e.tensor_tensor(
                            qmem[qs], tmps[qs][:, 0:128], g_mq, ALU.mult),
                             reads=[r_tmps[qs], r_par], writes=[r_qka[qs]])

                for n_ in range(NT + 2):
                    if n_ - 2 >= 0:
                        tail(n_ - 2)
                    if 0 <= n_ - 1 < NT:
                        stepB(n_ - 1)
                    if n_ < NT:
                        stepA(n_)
                    yield

            fss_i = [0]
            deferred = []

            def defer(n, fn):
                deferred.append([n, fn])

            def tick(flush=False):
                for d_ in list(deferred):
                    d_[0] -= 1
                    if d_[0] <= 0 or flush:
                        deferred.remove(d_)
                        d_[1]()

            def attn_gen(ui):
                kind, idx = units[ui]
                ub = ui % 2
                if kind == "da":
                    h = idx
                    for m in range(2):
                        def fin(tile, i, m=m, h=h):
                            rc, r_rc = get_small()
                            k.op("dve", lambda e: e.reciprocal(rc[:, 0:1], psf(i)[:, 128:129]),
                                 reads=[PS[i]], writes=[r_rc])
                            if m == 0:
                                k.op("dve", lambda e: e.tensor_scalar(O0n[:, tile, :], psf(i)[:, 0:128], rc[:, 0:1], None, ALU.mult),
                                     reads=[PS[i], r_rc], writes=[r_O0n[tile]])
                            else:
                                k.op("dve", lambda e: e.tensor_tensor(rc[:, 1:2], rc[:, 0:1], neg_lam, ALU.mult),
                                     reads=[r_rc, r_lam], writes=[r_rc])
                                k.op("dve", lambda e: e.scalar_tensor_tensor(
                                    O0n[:, tile, :], psf(i)[:, 0:128], rc[:, 1:2], O0n[:, tile, :], ALU.mult, ALU.add),
                                     reads=[PS[i], r_rc], writes=[r_O0n[tile]])
                                fi = fss_i[0] % 16
                                fss_i[0] += 1

                                def stB1():
                                    k.op("act", lambda e: e.activation(finjunk, O0n[:, tile, :], AF.Square, accum_out=fss[:, fi:fi + 1]),
                                         reads=[r_O0n[tile]], writes=[r_finjunk, r_fss[fi]])
                                    rstd_from_ss(fss[:, fi:fi + 1], r_fss[fi], 1, 1.0 / 128)

                                def stB2():
                                    k.op("dve", lambda e: e.scalar_tensor_tensor(
                                        o_br[0][:, tile, h * 128:(h + 1) * 128], O0n[:, tile, :], fss[:, fi:fi + 1], g_sub,
                                        ALU.mult, ALU.mult),
                                         reads=[r_O0n[tile], r_fss[fi], r_par], writes=[r_obr[0][tile]])
                                defer(2, stB1)
                                defer(4, stB2)
                        yield from attention(lambda c0, c1, m=m: qkT[ub][0:65, m, c0:c1], r_qT[ub],
                                             lambda j, m=m: qkT[ub][0:65, 2 + m, j * 128:(j + 1) * 128], r_qT[ub],
                                             lambda g: 4 * g + 4, True,
                                             lambda j, h=h: cst[:, C_ALK + j * 4 + h:C_ALK + j * 4 + h + 1],
                                             cst[:, C_BDA + 128 * h:C_BDA + 128 * (h + 1)],
                                             lambda j: Va[ub][:, j, 0:129], r_Va[ub], 129, 0.125, fin)
                    tick(flush=True)
                elif kind == "fx":
                    for m in range(2):
                        hd = 2 * idx + m

                        def fin(tile, i, hd=hd):
                            rc, r_rc = get_small()
                            k.op("dve", lambda e: e.reciprocal(rc[:, 0:1], psf(i)[:, 64:65]), reads=[PS[i]], writes=[r_rc])
                            k.op("dve", lambda e: e.tensor_scalar(o_br[1][:, tile, hd * 64:(hd + 1) * 64], psf(i)[:, 0:64],
                                                                  rc[:, 0:1], None, ALU.mult),
                                 reads=[PS[i], r_rc], writes=[r_obr[1][tile]])
                        yield from attention(lambda c0, c1, m=m: qkT[ub][0:65, m, c0:c1], r_qT[ub],
                                             lambda j, m=m: qkT[ub][0:65, 2 + m, j * 128:(j + 1) * 128], r_qT[ub],
                                             lambda g: 4 * g + 4, True,
                                             lambda j, hd=hd: negc[:, j, hd:hd + 1],
                                             cst[:, C_BFX:C_BFX + 128],
                                             lambda j, m=m: Va[ub][:, j, m * 65:(m + 1) * 65], r_Va[ub], 65, 0.125, fin)
                else:
                    h = idx

                    def fin(tile, i, h=h):
                        rc, r_rc = get_small()
                        k.op("dve", lambda e: e.reciprocal(rc[:, 0:1], psf(i)[:, 128:129]), reads=[PS[i]], writes=[r_rc])
                        k.op("dve", lambda e: e.tensor_scalar(o_br[2][:, tile, h * 128:(h + 1) * 128], psf(i)[:, 0:128],
                                                              rc[:, 0:1], None, ALU.mult),
                             reads=[PS[i], r_rc], writes=[r_obr[2][tile]])
                    yield from attention(lambda c0, c1: qkT[ub][:, 0, c0:c1], r_qT[ub],
                                         lambda j, h=h: kmT[:, h, j * 128:(j + 1) * 128], [r_kmT, r_kmT],
                                         lambda g: 2, False, None, None,
                                         lambda j, h=h: Vm[:, j, h * 129:(h + 1) * 129], [r_Vm, r_Vm], 129, 128 ** -0.5, fin)

            def run(gen):
                for _ in gen:
                    pass

            prev = None
            for ui in range(len(units)):
                pg = proj_gen(ui)
                if prev is None or not OPT_INTERLEAVE:
                    if prev is not None:
                        run(prev)
                    run(pg)
                else:
                    na = 88 if units[ui - 1][0] != "mem" else 12
                    for i_ in range(NT):
                        for _ in range(na * (i_ + 1) // NT - na * i_ // NT):
                            next(prev, None)
                        next(pg, None)
                    run(prev)
                    run(pg)
                prev = attn_gen(ui) if not SKIP_ATTN else None
            if prev is not None:
                run(prev)

            if dbg and s == 0:
                dump(o_br[0][:, 0, :], r_obr[0], 512)
                dump(o_br[1][:, 0, :], r_obr[1], 512)
                dump(o_br[2][:, 0, :], r_obr[2], 512)
                dump(o_br[0][:, 15, :], r_obr[0], 512)
                dump(o_br[1][:, 15, :], r_obr[1], 512)
                dump(o_br[2][:, 15, :], r_obr[2], 512)
            k.barrier()

            if STOP_AFTER <= 1:
                continue
            RB.reset(); RC.reset()
            merged = RB.alloc([NT, D], BF16)
            r_mg = [k.res("mg%d" % t) for t in range(NT)]
            RA2 = Bump(arena, RA0 + 48 * 1024, RB0)
            Wg = RC.alloc([8, 3 * 512], BF16)
            r_Wg = [k.res("Wg%d" % i, dma=True) for i in range(3)]
            wb = RC.alloc([4, 3 * 512], BF16)
            r_wb = [k.res("wb%d" % i, dma=True) for i in range(3)]
            bg = RC.alloc([3 * D], F32)
            r_bg = k.res("bg", dma=True)
            oT = [RA2.alloc([12, 128], BF16) for _ in range(2)]
            r_oT = [k.res("oT%d" % i) for i in range(2)]
            gtmp = [RC.alloc([512], F32) for _ in range(2)]
            r_gtmp = [k.res("gtmp%d" % i) for i in range(2)]
            pr = [RC.alloc([512], F32) for _ in range(3)]
            r_pr = [k.res("pr%d" % i) for i in range(3)]
            for i in range(3):
                k.dma(AQ, bg[:, i * D:(i + 1) * D], P["b_gate"][0, i:i + 1, :].partition_broadcast(128),
                      writes=[r_bg], sem_res=r_bg)
            wbn = ["w_branch_da", "w_branch_fx", "w_branch_mem"]
            gi = [0]
            for n in range(2):
                for i in range(3):
                    wload(wb[:, :, i * 512:(i + 1) * 512], r_wb[i],
                          P[wbn[i]][0, :, n * 512:(n + 1) * 512].rearrange("(c p) n -> p c n", p=128))
                    wload(Wg[:, :, i * 512:(i + 1) * 512], r_Wg[i], win_cols(O_G + i * D + n * 512, 512))
                def emit_T(t):
                    osl = t % 2
                    for rnd in range(2):
                        pbk = rnd
                        lo, hi = (0, 8) if rnd == 0 else (8, 12)
                        for cc in range(lo, hi):
                            br, c4 = cc // 4, cc % 4
                            k.op("pe", lambda e, cc=cc, br=br, c4=c4: e.transpose(
                                psh(pbk)[:, (cc - lo) * 128:(cc - lo + 1) * 128],
                                o_br[br][:, t, c4 * 128:(c4 + 1) * 128], identb),
                                 reads=[r_obr[br][t], r_idb], writes=[PS[pbk]], signal=(cc == hi - 1), accum=True)
                        k.op("act", lambda e: e.copy(
                            oT[osl][:, lo:hi, :], psh(pbk)[:, 0:(hi - lo) * 128].rearrange("p (c t) -> p c t", c=hi - lo)),
                             reads=[PS[pbk]], writes=[r_oT[osl]])

                emit_T(0)
                for t in range(NT):
                    osl = t % 2
                    if t + 1 < NT:
                        emit_T(t + 1)
                    for i in range(3):
                        yb, gb = 2 + i, 5 + i
                        for c in range(4):
                            k.op("pe", lambda e, c=c, i=i, yb=yb, osl=osl: e.matmul(
                                psf(yb), oT[osl][:, 4 * i + c, :], wb[:, c, i * 512:(i + 1) * 512],
                                start=(c == 0), stop=(c == 3)),
                                 reads=[r_oT[osl], r_wb[i]], writes=[PS[yb]], signal=(c == 3), accum=True)
                        for c in range(8):
                            k.op("pe", lambda e, c=c, i=i, gb=gb, t=t: e.matmul(
                                psf(gb), hT[:, c, t * 128:(t + 1) * 128], Wg[:, c, i * 512:(i + 1) * 512],
                                start=(c == 0), stop=(c == 7)),
                                 reads=[hT_res[t], r_Wg[i]], writes=[PS[gb]], signal=(c == 7), accum=True)
                        gs = gi[0] % 2
                        gi[0] += 1
                        k.op("dve", lambda e, gs=gs, gb=gb, i=i, n=n: e.tensor_tensor(
                            gtmp[gs], psf(gb), bg[:, i * D + n * 512:i * D + (n + 1) * 512], ALU.add),
                             reads=[PS[gb], r_bg], writes=[r_gtmp[gs]])
                        k.op("act", lambda e, gs=gs: e.activation(gtmp[gs], gtmp[gs], AF.Sigmoid),
                             reads=[r_gtmp[gs]], writes=[r_gtmp[gs]])
                        k.op("dve", lambda e, gs=gs, yb=yb, i=i: e.tensor_tensor(pr[i], gtmp[gs], psf(yb), ALU.mult),
                             reads=[r_gtmp[gs], PS[yb]], writes=[r_pr[i]])
                    k.op("pool", lambda e: e.tensor_tensor(pr[0], pr[0], pr[1], ALU.add),
                         reads=[r_pr[0], r_pr[1]], writes=[r_pr[0]])
                    k.op("pool", lambda e, t=t, n=n: e.tensor_tensor(merged[:, t, n * 512:(n + 1) * 512], pr[0], pr[2], ALU.add),
                         reads=[r_pr[0], r_pr[2]], writes=[r_mg[t]])
            if dbg and s == 0:
                dump(merged[:, 0, :], r_mg, 1024)
            k.barrier()

            RA.reset(); RC.reset()
            x1 = RA.alloc([NT, D], F32)
            r_x1 = [k.res("x1_%d" % t, dma=True) for t in range(NT)]
            wo = RC.alloc([8, D], BF16)
            r_wo = k.res("wo", dma=True)
            xt = [RC.alloc([D], F32) for _ in range(2)]
            r_xt = [k.res("xt3_%d" % i, dma=True) for i in range(2)]
            xn = [RC.alloc([D], BF16) for _ in range(2)]
            r_xn = [k.res("xn3_%d" % i) for i in range(2)]
            sqb3 = RC.alloc([D], F32)
            r_sqb3 = k.res("sqb3")
            mT = [RC.alloc([8, 128], BF16) for _ in range(2)]
            r_mT = [k.res("mT%d" % i) for i in range(2)]
            wload(wo, r_wo, P["w_out"][0].rearrange("(c p) n -> p c n", p=128))
            for n_ in range(NT + 3):
                if n_ - 3 >= 0:
                    t = n_ - 3
                    rms_s3(xn[t % 2], r_xn[t % 2], nf_c, hT[:, :, t * 128:(t + 1) * 128], hT_res[t], 6 + t % 2)
                if 0 <= n_ - 2 < NT:
                    t = n_ - 2
                    rms_s2(t, x1[:, t, :], r_x1[t], xn[t % 2], r_xn[t % 2])
                if 0 <= n_ - 1 < NT:
                    t = n_ - 1
                    sl = t % 2
                    for n in range(2):
                        ob = 2 + 2 * sl + n
                        for c in range(8):
                            k.op("pe", lambda e, c=c: e.matmul(
                                psf(ob), mT[sl][:, c, :], wo[:, c, n * 512:(n + 1) * 512], start=(c == 0), stop=(c == 7)),
                                 reads=[r_mT[sl], r_wo], writes=[PS[ob]], signal=(c == 7), accum=True)
                        k.op("dve", lambda e: e.tensor_tensor(
                            x1[:, t, n * 512:(n + 1) * 512], psf(ob), xt[sl][:, n * 512:(n + 1) * 512], ALU.add),
                             reads=[PS[ob], r_xt[sl]], writes=[r_x1[t]])
                    rms_s1(t, x1[:, t, :], r_x1[t], sqb3, r_sqb3)
                if n_ < NT:
                    t = n_
                    sl = t % 2
                    k.dma(AQ, xt[sl], x_d[s, t * 128:(t + 1) * 128, :], writes=[r_xt[sl]], sem_res=r_xt[sl])
                    for c in range(8):
                        k.op("pe", lambda e, c=c: e.transpose(
                            psh(sl)[:, c * 128:(c + 1) * 128], merged[:, t, c * 128:(c + 1) * 128], identb),
                             reads=[r_mg[t], r_idb], writes=[PS[sl]], signal=(c == 7), accum=True)
                    k.op("act", lambda e: e.copy(mT[sl], psh(sl).rearrange("p (c t) -> p c t", c=8)),
                         reads=[PS[sl]], writes=[r_mT[sl]])
            if dbg and s == 0:
                dump(x1[:, 0, :], r_x1, 1024)
            k.barrier()

            if STOP_AFTER <= 3:
                continue
            RB.reset(); RC.reset()
            actT = RB.alloc([NFF, 512], BF16)
            r_actT = [k.res("actT%d" % p) for p in range(NFF)]
            ubuf = [RB.alloc([2, 514], F32) for _ in range(2)]
            r_ub = [[k.res("ub%d_%d" % (i, a)) for a in range(2)] for i in range(2)]
            wd = RC.alloc([NFF, D], BF16)
            r_wd = k.res("wd", dma=True)
            wup = [RC.alloc([2, 1024], BF16) for _ in range(2)]
            r_wup = [k.res("wup%d" % i, dma=True) for i in range(2)]
            uc = [[RC.alloc([512], F32) for _ in range(2)] for _ in range(2)]
            r_uc = [[k.res("uc%d_%d" % (i, a)) for a in range(2)] for i in range(2)]
            carry = RC.alloc([2 * NFF, 2], F32)
            r_carry = [k.res("carry%d" % i) for i in range(2 * NFF)]
            wload(wd, r_wd, P["w_down"][0].rearrange("(c p) n -> p c n", p=128))
            k.op("pool", lambda e: e.memset(carry, 0.0), writes=r_carry)
            cb_c, cw2_c, cw0_c, cw1_c = 24, 68, 112, 156
            wi = [0]
            pbi = [0]
            cti = [0]
            for tg in range(4):
                for p in range(NFF):
                    ws = wi[0] % 2
                    wi[0] += 1
                    us = p % 2
                    k.dma(AQ, wup[ws], wup_s[2 * p:2 * p + 2].rearrange("b p n -> p b n"),
                          reads=[r_wups[p]], writes=[r_wup[ws]], sem_res=r_wup[ws])
                    for part in range(2):
                        ch = p + NFF * part
                        pb = pbi[0] % 4
                        pbi[0] += 1
                        for c in range(8):
                            k.op("pe", lambda e, c=c: e.matmul(
                                psf(pb), wup[ws][:, part, c * 128:(c + 1) * 128], hT[:, c, tg * 512:(tg + 1) * 512],
                                start=(c == 0), stop=(c == 7)),
                                 reads=[r_wup[ws]] + hT_res[4 * tg:4 * tg + 4], writes=[PS[pb]], signal=(c == 7), accum=True)
                        ub_ = ubuf[us]
                        k.op("act", lambda e: e.copy(ub_[:, part, 0:2], carry[:, ch, :]),
                             reads=[r_carry[ch]], writes=[r_ub[us][part]])
                        k.op("act", lambda e: e.copy(ub_[:, part, 2:514], psf(pb)),
                             reads=[PS[pb]], writes=[r_ub[us][part]])
                        k.op("act", lambda e: e.copy(carry[:, ch, :], ub_[:, part, 512:514]),
                             reads=[r_ub[us][part]], writes=[r_carry[ch]])
                        k.op("act", lambda e: e.activation(uc[us][part], psf(pb), AF.Identity,
                                                           bias=pcol[:, cb_c + ch:cb_c + ch + 1],
                                                           scale=pcol[:, cw2_c + ch:cw2_c + ch + 1]),
                             reads=[PS[pb], r_pcol], writes=[r_uc[us][part]])
                        k.op("dve", lambda e: e.scalar_tensor_tensor(
                            uc[us][part], ub_[:, part, 1:513], pcol[:, cw1_c + ch:cw1_c + ch + 1], uc[us][part], ALU.mult, ALU.add),
                             reads=[r_ub[us][part], r_pcol], writes=[r_uc[us][part]])
                        k.op("dve", lambda e: e.scalar_tensor_tensor(
                            uc[us][part], ub_[:, part, 0:512], pcol[:, cw0_c + ch:cw0_c + ch + 1], uc[us][part], ALU.mult, ALU.add),
                             reads=[r_ub[us][part], r_pcol], writes=[r_uc[us][part]])
                    k.op("act", lambda e: e.activation(uc[us][0], uc[us][0], AF.Silu),
                         reads=[r_uc[us][0]], writes=[r_uc[us][0]])
                    k.op("dve", lambda e: e.tensor_tensor(actT[:, p, :], uc[us][0], uc[us][1], ALU.mult),
                         reads=[r_uc[us][0], r_uc[us][1]], writes=[r_actT[p]])
                for tt in range(4):
                    t = tg * 4 + tt
                    osl = t % 2
                    for n in range(2):
                        ob = 4 + (2 * t + n) % 4
                        for p in range(NFF):
                            k.op("pe", lambda e, p=p, n=n, ob=ob, tt=tt: e.matmul(
                                psf(ob), actT[:, p, tt * 128:(tt + 1) * 128], wd[:, p, n * 512:(n + 1) * 512],
                                start=(p == 0), stop=(p == NFF - 1)),
                                 reads=[r_actT[p], r_wd], writes=[PS[ob]], signal=(p == NFF - 1), accum=True)
                        k.op("dve", lambda e, n=n, ob=ob, t=t: e.tensor_tensor(
                            x1[:, t, n * 512:(n + 1) * 512], psf(ob), x1[:, t, n * 512:(n + 1) * 512], ALU.add),
                             reads=[PS[ob]], writes=[r_x1[t]])
                    k.dma(AQ, out_d[s, t * 128:(t + 1) * 128, :], x1[:, t, :], reads=[r_x1[t]], sem_res=r_x1[t])
            k.barrier()
        k.final()
    return nc


_NC_CACHE = {}


def kernel(**inputs):
    n = 8
    x = np.ascontiguousarray(inputs["x"], dtype=np.float32)
    mem = np.ascontiguousarray(inputs["mem"], dtype=np.float32)
    B = x.shape[0]
    per = B // n
    if "nc" not in _NC_CACHE:
        _NC_CACHE["nc"] = build(per)
    nc = _NC_CACHE["nc"]
    cst = make_consts()
    params = {nme: np.ascontiguousarray(inputs[nme], dtype=np.float32) for nme in PARAM_NAMES}
    in_maps = []
    for i in range(n):
        m = {"x": x[i * per:(i + 1) * per], "mem": mem[i * per:(i + 1) * per], "cst": cst}
        m.update(params)
        in_maps.append(m)
    res = run_bass_kernel_spmd(nc, in_maps, core_ids=list(range(n)))
    return np.concatenate([r["out"] for r in res.results], axis=0)
```

```python
import math
from contextlib import ExitStack

import numpy as np
import concourse.bass as bass
import concourse.mybir as mybir
from concourse.bass_utils import run_bass_kernel_spmd

F32 = mybir.dt.float32
BF16 = mybir.dt.bfloat16
U8 = mybir.dt.uint8
AF = mybir.ActivationFunctionType
ALU = mybir.AluOpType
AX = mybir.AxisListType

D = 1024
S = 2048
NT = S // 128
NMEM = 256
IN_COLS = 6664
DFF = 2816
NFF = DFF // 128
EPS = 1e-6
SLOPES = [2.0 ** (-8.0 * (i + 1) / 4) for i in range(4)]
LAM_INIT = 0.8 - 0.6 * math.exp(-0.3 * 0)
NEG = -30000.0
import os
OPT_INTERLEAVE = int(os.environ.get('OPT_INTERLEAVE', '1'))
MAX_INFLIGHT = int(os.environ.get('MAX_INFLIGHT', '12'))
STOP_AFTER = int(os.environ.get('STOP_AFTER', '99'))
NO_WAW_SKIP = int(os.environ.get('NO_WAW_SKIP', '0'))
UNITS_LIMIT = int(os.environ.get('UNITS_LIMIT', '12'))
SKIP_ATTN = int(os.environ.get('SKIP_ATTN', '0'))
PROJ_STAGE = int(os.environ.get('PROJ_STAGE', '9'))
DUMMY_MM = int(os.environ.get('DUMMY_MM', '0'))
O_AQ, O_AK, O_AV, O_FQ, O_FK, O_FV, O_FG, O_MQ, O_G = 0, 512, 1024, 1536, 2048, 2560, 3072, 3080, 3592

C_ID, C_U, C_ONES, C_BDA, C_BFX, C_ALK, C_DAQ, C_END = 0, 128, 256, 384, 896, 1024, 1088, 1152

PARAM_NAMES = ["norm_mix", "w_in", "b_gate", "da_q_norm", "da_k_norm", "da_lambda_q1", "da_lambda_k1",
               "da_lambda_q2", "da_lambda_k2", "da_subln", "fx_q_norm", "fx_k_norm", "fx_f_bias",
               "mem_norm", "w_mem_kv", "mem_q_norm", "mem_k_norm", "w_branch_da", "w_branch_fx",
               "w_branch_mem", "w_out", "norm_ffn", "w_up", "conv_w", "conv_b", "w_down"]


def make_consts():
    c = np.zeros((128, C_END), np.float32)
    p = np.arange(128)
    c[:, C_ID:C_ID + 128] = np.eye(128, dtype=np.float32)
    c[:, C_U:C_U + 128] = (p[:, None] <= p[None, :]).astype(np.float32)
    c[:, C_ONES:C_ONES + 128] = 1.0
    k = p[:, None]
    q = p[None, :]
    for h in range(4):
        b = np.where(k <= q, 0.0,
                     np.where((k // 64) == (q // 64), -2.0 * SLOPES[h] * (k - q), NEG))
        c[:, C_BDA + 128 * h:C_BDA + 128 * (h + 1)] = b
    c[:, C_BFX:C_BFX + 128] = np.where(k <= q, 0.0, NEG)
    for j in range(NT):
        for h in range(4):
            pos = 128 * j + p
            c[:, C_ALK + j * 4 + h] = SLOPES[h] * pos
            c[:, C_DAQ + j * 4 + h] = -SLOPES[h] * pos * 8.0
    return c


class DSem:
    def __init__(self, h):
        self.h = h
        self.count = 0


class Res:
    __slots__ = ("name", "w", "r", "dsem", "psum")

    def __init__(self, name, dsem=None):
        self.name = name
        self.psum = False
        self.w = None
        self.r = {}
        self.dsem = dsem


class K:
    def __init__(self, nc, es):
        self.nc = nc
        self.es = es
        self.eng = {"pe": nc.tensor, "act": nc.scalar, "dve": nc.vector, "pool": nc.gpsimd, "sp": nc.sync}
        self.sem = {}
        self.cnt = {}
        self.seen = {}
        for e in self.eng:
            self.sem[e] = es.enter_context(nc.semaphore("sem_" + e))
            self.cnt[e] = 0
            self.seen[e] = {}
        self.dsems = []
        self.nres = 0
        self.inflight = {}

    def res(self, name, dma=False):
        self.nres += 1
        ds = None
        if dma:
            ds = DSem(self.es.enter_context(self.nc.semaphore("d_%s_%d" % (name, self.nres))))
            self.dsems.append(ds)
        return Res(name, ds)

    def wait(self, e, tok):
        if tok is None:
            return
        sem, val = tok
        key = sem.num
        if self.seen[e].get(key, 0) >= val:
            return
        self.eng[e].wait_ge(sem, val)
        self.seen[e][key] = val

    def _deps(self, e, reads, writes, accum):
        for r in reads:
            self.wait(e, r.w)
            if r.psum:
                for t in list(r.r.values()):
                    if t[0] is not self.sem[e]:
                        self.wait(e, t)
        for w in writes:
            if not (accum and w.w is not None and w.w[0] is self.sem[e]):
                self.wait(e, w.w)
            for t in list(w.r.values()):
                self.wait(e, t)

    def _record(self, tok, reads, writes):
        for r in reads:
            old = r.r.get(tok[0].num)
            if old is None or old[1] < tok[1]:
                r.r[tok[0].num] = tok
        for w in writes:
            w.w = tok
            w.r = {}

    def op(self, e, fn, reads=(), writes=(), signal=True, accum=False):
        self._deps(e, reads, writes, accum)
        ins = fn(self.eng[e])
        if signal:
            self.cnt[e] += 1
            ins.then_inc(self.sem[e], 1)
            tok = (self.sem[e], self.cnt[e])
        else:
            tok = (self.sem[e], self.cnt[e] + 1)
        self._record(tok, reads, writes)
        return ins

    def dma(self, q, out, in_, reads=(), writes=(), sem_res=None, **kw):
        ds = sem_res.dsem
        for r in reads:
            self.wait(q, r.w)
        for w in writes:
            if NO_WAW_SKIP or not (w.w is not None and w.w[0] is ds.h):
                self.wait(q, w.w)
            for t in list(w.r.values()):
                self.wait(q, t)
        fl = self.inflight.setdefault(q, [])
        if len(fl) >= MAX_INFLIGHT:
            ds_old = fl.pop(0)
            self.wait(q, (ds_old.h, ds_old.count))
        ins = self.eng[q].dma_start(out=out, in_=in_, **kw)
        ds.count += 16
        ins.then_inc(ds.h, 16)
        tok = (ds.h, ds.count)
        fl.append(ds)
        self._record(tok, reads, writes)
        return ins

    def barrier(self):
        toks = [(self.sem[e], self.cnt[e]) for e in self.eng if self.cnt[e] > 0]
        toks += [(d.h, d.count) for d in self.dsems if d.count > 0]
        for e in self.eng:
            for t in toks:
                self.wait(e, t)

    def final(self):
        toks = [(d.h, d.count) for d in self.dsems if d.count > 0]
        toks += [(self.sem[e], self.cnt[e]) for e in self.eng if self.cnt[e] > 0]
        for t in toks:
            self.wait("sp", t)


class Bump:
    def __init__(self, arena, lo, hi):
        self.arena, self.lo, self.hi, self.cur = arena, lo, hi, lo

    def reset(self):
        self.cur = self.lo

    def alloc(self, shape, dt):
        n = int(np.prod(shape))
        sz = n * (4 if dt == F32 else 2)
        off = (self.cur + 31) // 32 * 32
        assert off + sz <= self.hi, ("arena overflow", off, sz, self.hi)
        self.cur = off + sz
        ap = self.arena[:, off:off + sz].bitcast(dt)
        if len(shape) == 2:
            ap = ap.rearrange("p (a b) -> p a b", a=shape[0])
        elif len(shape) == 3:
            ap = ap.rearrange("p (a b c) -> p a b c", a=shape[0], b=shape[1])
        return ap


def build(nseq=2, dbg=False):
    nc = bass.Bass("TRN2", target_bir_lowering=False)
    x_d = nc.dram_tensor("x", [nseq, S, D], F32, kind="ExternalInput").ap()
    mem_d = nc.dram_tensor("mem", [nseq, NMEM, D], F32, kind="ExternalInput").ap()
    cst_d = nc.dram_tensor("cst", [128, C_END], F32, kind="ExternalInput").ap()
    shp = {"norm_mix": [1, D], "w_in": [1, D, IN_COLS], "b_gate": [1, 3, D], "da_q_norm": [1, 64],
           "da_k_norm": [1, 64], "da_lambda_q1": [1, 64], "da_lambda_k1": [1, 64], "da_lambda_q2": [1, 64],
           "da_lambda_k2": [1, 64], "da_subln": [1, 128], "fx_q_norm": [1, 64], "fx_k_norm": [1, 64],
           "fx_f_bias": [1, 8], "mem_norm": [1, D], "w_mem_kv": [1, D, 1024], "mem_q_norm": [1, 128],
           "mem_k_norm": [1, 128], "w_branch_da": [1, 512, D], "w_branch_fx": [1, 512, D],
           "w_branch_mem": [1, 512, D], "w_out": [1, D, D], "norm_ffn": [1, D], "w_up": [1, D, 2 * DFF],
           "conv_w": [1, 3, 2 * DFF], "conv_b": [1, 2 * DFF], "w_down": [1, DFF, D]}
    P = {n: nc.dram_tensor(n, shp[n], F32, kind="ExternalInput").ap() for n in PARAM_NAMES}
    out_d = nc.dram_tensor("out", [nseq, S, D], F32, kind="ExternalOutput").ap()
    wup_s = nc.dram_tensor("wup_s", [2 * NFF, 128, 1024], BF16).ap()
    dbg_d = nc.dram_tensor("dbg", [128, 8192], F32, kind="ExternalOutput").ap() if dbg else None

    WQ = "pool"
    AQ = "sp"

    with ExitStack() as es:
        ARENA = 207 * 1024
        arena = es.enter_context(nc.sbuf_tensor("arena", [128, ARENA], U8))
        psb = [es.enter_context(nc.psum_tensor("ps%d" % i, [128, 512], F32)) for i in range(8)]
        k = K(nc, es)
        PS = [k.res("ps%d" % i) for i in range(8)]
        for r_ in PS:
            r_.psum = True

        def psf(i):
            return psb[i][:, :]

        def psh(i):
            return psb[i][:, :].bitcast(BF16)

        CB = Bump(arena, 0, 16 * 1024)
        HT0 = 16 * 1024
        RA0 = HT0 + 32 * 1024
        RB0 = RA0 + 64 * 1024
        RC0 = RB0 + 32 * 1024
        hT = Bump(arena, HT0, RA0).alloc([8, S], BF16)
        RA = Bump(arena, RA0, RB0)
        RB = Bump(arena, RB0, RC0)
        RC = Bump(arena, RC0, ARENA)
        hT_res = [k.res("hT%d" % t) for t in range(NT)]
        r_wups = [k.res("wups%d" % p, dma=True) for p in range(NFF)]

        cst = CB.alloc([C_END], F32)
        r_cst = k.res("cst", dma=True)
        k.dma(AQ, cst, cst_d, writes=[r_cst], sem_res=r_cst)
        identb = CB.alloc([128], BF16)
        r_idb = k.res("identb", dma=True)
        k.dma(WQ, identb, cst_d[:, C_ID:C_ID + 128], writes=[r_idb], sem_res=r_idb)
        identf = cst[:, C_ID:C_ID + 128]
        Umat = cst[:, C_U:C_U + 128]
        ones_f = cst[:, C_ONES:C_ONES + 128]

        r_par = k.res("params", dma=True)

        def bload(name, n, reps=1):
            if isinstance(name, str):
                name = [name] * reps
            t = CB.alloc([n * len(name)], F32)
            for r, nm in enumerate(name):
                k.dma(AQ, t[:, r * n:(r + 1) * n], P[nm][0:1, :].partition_broadcast(128),
                      writes=[r_par], sem_res=r_par)
            return t

        g_da4 = bload(["da_q_norm", "da_q_norm", "da_k_norm", "da_k_norm"], 64)
        g_fx4 = bload(["fx_q_norm", "fx_q_norm", "fx_k_norm", "fx_k_norm"], 64)
        g_mq = bload("mem_q_norm", 128)
        g_mk4 = bload("mem_k_norm", 128, 4)
        g_sub = bload("da_subln", 128)
        fbias = bload("fx_f_bias", 8)
        lq1 = bload("da_lambda_q1", 64)
        lk1 = bload("da_lambda_k1", 64)
        lq2 = bload("da_lambda_q2", 64)
        lk2 = bload("da_lambda_k2", 64)

        rows1 = CB.alloc([128], F32)
        rows2 = CB.alloc([128], F32)
        r_rows = k.res("rows", dma=True)
        k.dma(AQ, rows1[0:8, :], P["norm_mix"][0].rearrange("(c p) -> c p", p=128), writes=[r_rows], sem_res=r_rows)
        k.dma(AQ, rows1[8:16, :], P["norm_ffn"][0].rearrange("(c p) -> c p", p=128), writes=[r_rows], sem_res=r_rows)
        k.dma(AQ, rows1[16:24, :], P["mem_norm"][0].rearrange("(c p) -> c p", p=128), writes=[r_rows], sem_res=r_rows)
        k.dma(AQ, rows1[24:68, :], P["conv_b"][0].rearrange("(c p) -> c p", p=128), writes=[r_rows], sem_res=r_rows)
        k.dma(AQ, rows1[68:112, :], P["conv_w"][0, 2].rearrange("(c p) -> c p", p=128), writes=[r_rows], sem_res=r_rows)
        k.dma(AQ, rows2[0:44, :], P["conv_w"][0, 0].rearrange("(c p) -> c p", p=128), writes=[r_rows], sem_res=r_rows)
        k.dma(AQ, rows2[44:88, :], P["conv_w"][0, 1].rearrange("(c p) -> c p", p=128), writes=[r_rows], sem_res=r_rows)
        pcol = CB.alloc([200], F32)
        r_pcol = k.res("pcol")
        k.op("pe", lambda e: e.matmul(psf(0)[:, 0:112], rows1[0:112, :], identf[0:112, 0:112], start=True, stop=True),
             reads=[r_rows, r_cst], writes=[PS[0]])
        k.op("pe", lambda e: e.matmul(psf(0)[:, 112:200], rows2[0:88, :], identf[0:88, 0:88], start=True, stop=True),
             reads=[r_rows, r_cst], writes=[PS[0]], accum=True)
        k.op("dve", lambda e: e.tensor_copy(pcol, psf(0)[:, 0:200]), reads=[PS[0]], writes=[r_pcol])
        nm_c, nf_c, mn_c = pcol[:, 0:8], pcol[:, 8:16], pcol[:, 16:24]

        lam_t = CB.alloc([8], F32)
        r_lam = k.res("lam")
        junk = CB.alloc([64], F32)
        r_junk = k.res("junk")
        k.op("dve", lambda e: e.tensor_tensor(junk, lq1, lk1, ALU.mult), reads=[r_par], writes=[r_junk])
        k.op("dve", lambda e: e.tensor_reduce(lam_t[:, 0:1], junk, AX.X, ALU.add), reads=[r_junk], writes=[r_lam])
        k.op("dve", lambda e: e.tensor_tensor(junk, lq2, lk2, ALU.mult), reads=[r_par, r_lam], writes=[r_junk])
        k.op("dve", lambda e: e.tensor_reduce(lam_t[:, 1:2], junk, AX.X, ALU.add), reads=[r_junk], writes=[r_lam])
        k.op("act", lambda e: e.activation(lam_t[:, 2:4], lam_t[:, 0:2], AF.Exp), reads=[r_lam], writes=[r_lam])
        k.op("dve", lambda e: e.tensor_tensor(lam_t[:, 4:5], lam_t[:, 2:3], lam_t[:, 3:4], ALU.subtract),
             reads=[r_lam], writes=[r_lam])
        k.op("dve", lambda e: e.tensor_scalar(lam_t[:, 5:6], lam_t[:, 4:5], -1.0, -LAM_INIT, ALU.mult, ALU.add),
             reads=[r_lam], writes=[r_lam])
        neg_lam = lam_t[:, 5:6]
        k.op("dve", lambda e: e.tensor_scalar(g_sub, g_sub, 1.0 - LAM_INIT, None, ALU.mult),
             reads=[r_par], writes=[r_par])
        zall = CB.alloc([NT, 8], F32)
        logf = CB.alloc([NT, 8], F32)
        negc = CB.alloc([NT, 8], F32)
        c8 = CB.alloc([NT, 8], F32)
        r_z, r_logf, r_negc, r_c8 = k.res("z"), k.res("logf"), k.res("negc"), k.res("c8")
        pref = zall
        r_pref = r_z
        small = CB.alloc([128], F32)
        r_small = [k.res("small%d" % i) for i in range(16)]
        small_i = [0]

        def get_small():
            i = small_i[0] % 16
            small_i[0] += 1
            return small[:, i * 8:(i + 1) * 8], r_small[i]

        def rstd_from_ss(ss_ap, r_ss, n, inv_d):
            k.op("act", lambda e: e.activation(ss_ap, ss_ap, AF.Ln, bias=eps_c, scale=inv_d),
                 reads=[r_ss, r_par], writes=[r_ss])
            k.op("act", lambda e: e.activation(ss_ap, ss_ap, AF.Exp, scale=-0.5), reads=[r_ss], writes=[r_ss])

        eps_c = CB.alloc([1], F32)
        k.op("pool", lambda e: e.memset(eps_c, EPS), writes=[r_par])

        rss = CB.alloc([8], F32)
        r_rss = [k.res("rss%d" % i) for i in range(4)]

        def rms_s1(t, src_ap, r_src, sq_ap, r_sq):
            ss = rss[:, t % 4:t % 4 + 1]
            k.op("act", lambda e: e.activation(sq_ap, src_ap, AF.Square, accum_out=ss),
                 reads=[r_src], writes=[r_sq, r_rss[t % 4]])

        def rms_s2(t, src_ap, r_src, xn_ap, r_xn):
            ss = rss[:, t % 4:t % 4 + 1]
            rstd_from_ss(ss, r_rss[t % 4], 1, 1.0 / D)
            k.op("act", lambda e: e.activation(xn_ap, src_ap, AF.Copy, scale=ss),
                 reads=[r_src, r_rss[t % 4]], writes=[r_xn])

        def rms_s3(xn_ap, r_xn, gain_cols, dstT, r_dst, pbank):
            for c in range(8):
                k.op("pe", lambda e, c=c: e.transpose(psh(pbank)[:, c * 128:(c + 1) * 128],
                                                      xn_ap[:, c * 128:(c + 1) * 128], identb),
                     reads=[r_xn, r_idb], writes=[PS[pbank]], signal=(c == 7), accum=True)
            k.op("dve", lambda e: e.tensor_tensor(dstT, psh(pbank).rearrange("p (c t) -> p c t", c=8),
                                                  gain_cols.unsqueeze(2).broadcast_to([128, 8, 128]), ALU.mult),
                 reads=[PS[pbank], r_pcol], writes=[r_dst])

        def rms_to_T(src_ap, r_src, xn_ap, r_xn, sq_ap, r_sq, gain_cols, dstT, r_dst, pbank, t=0):
            rms_s1(t, src_ap, r_src, sq_ap, r_sq)
            rms_s2(t, src_ap, r_src, xn_ap, r_xn)
            rms_s3(xn_ap, r_xn, gain_cols, dstT, r_dst, pbank)

        def qknorm(ps_ap, r_ps, H, d, gain_ap, out_ap, r_out, sq_ap, r_sq, tmp_ap, r_tmp):
            ss, r_ss = get_small()
            k.op("act", lambda e: e.activation(sq_ap[:, 0:H * d], ps_ap, AF.Square), reads=[r_ps], writes=[r_sq])
            k.op("dve", lambda e: e.tensor_reduce(ss[:, 0:H], sq_ap[:, 0:H * d].rearrange("p (h d) -> p h d", h=H),
                                                  AX.X, ALU.add), reads=[r_sq], writes=[r_ss])
            rstd_from_ss(ss[:, 0:H], r_ss, H, 1.0 / d)
            k.op("dve", lambda e: e.tensor_tensor(tmp_ap[:, 0:H * d].rearrange("p (h d) -> p h d", h=H),
                                                  ps_ap.rearrange("p (h d) -> p h d", h=H),
                                                  ss[:, 0:H].unsqueeze(2).broadcast_to([128, H, d]), ALU.mult),
                 reads=[r_ps, r_ss], writes=[r_tmp])
            k.op("dve", lambda e: e.tensor_tensor(out_ap, tmp_ap[:, 0:H * d].rearrange("p (h d) -> p h d", h=H),
                                                  gain_ap.rearrange("p (h d) -> p h d", h=H), ALU.mult),
                 reads=[r_tmp, r_par], writes=[r_out])

        def wload(dst, r_dst, src, eng=None):
            k.dma(eng or WQ, dst, src, writes=[r_dst], sem_res=r_dst)

        def win_cols(c0, n):
            return P["w_in"][0, :, c0:c0 + n].rearrange("(c p) n -> p c n", p=128)

        dbg_n = [0]

        def dump(ap, reads, ncols):
            if not dbg:
                return
            r = k.res("dbg", dma=True)
            k.dma("pool", dbg_d[:, dbg_n[0]:dbg_n[0] + ncols], ap, reads=reads, writes=[r], sem_res=r)
            dbg_n[0] += ncols

        for s in range(nseq):
            RA.reset(); RB.reset(); RC.reset()
            o_br = [RA.alloc([NT, 512], BF16) for _ in range(3)]
            r_obr = [[k.res("o%d_%d" % (b, t)) for t in range(NT)] for b in range(3)]
            xt = [RA.alloc([D], F32) for _ in range(2)]
            r_xt = [k.res("xt%d" % i, dma=True) for i in range(2)]
            xn = [RA.alloc([D], BF16) for _ in range(2)]
            r_xn = [k.res("xn%d" % i) for i in range(2)]
            wkv = Bump(arena, RC0 + 13 * 1024, ARENA).alloc([8, 1024], BF16)
            r_wkv = k.res("wkv", dma=True)
            mhT = RB.alloc([8, NMEM], BF16)
            r_mhT = [k.res("mhT%d" % i) for i in range(2)]
            kmT = RB.alloc([4, NMEM], BF16)
            r_kmT = k.res("kmT")
            Vm = RB.alloc([2, 4 * 129], BF16)
            r_Vm = k.res("Vm")
            kbm = RB.alloc([4, 128], BF16)
            r_kbm = k.res("kbm")
            wu = [RC.alloc([8, 384], BF16) for _ in range(2)]
            r_wu = [k.res("wu%d" % i, dma=True) for i in range(2)]
            qkT = [RC.alloc([4, S], BF16) for _ in range(2)]
            Va = [RC.alloc([NT, 130], BF16) for _ in range(2)]
            r_qT = [[k.res("qkT%d_%d" % (i, t)) for t in range(NT)] for i in range(2)]
            r_Va = [[k.res("Va%d_%d" % (i, t)) for t in range(NT)] for i in range(2)]
            qka = [RA.alloc([4, 65], BF16) for _ in range(2)]
            r_qka = [k.res("qka%d" % i) for i in range(2)]
            qmem = [RA.alloc([128], BF16) for _ in range(2)]
            praw = [RA.alloc([256], F32) for _ in range(2)]
            r_praw = [k.res("praw%d" % i) for i in range(2)]
            sqj = RB.alloc([D], F32)
            r_sq512 = k.res("sqj")
            r_tmp512 = r_sq512
            sq512 = sqj[:, 0:512]
            tmp512 = sqj[:, 512:1024]
            sqb = [sqj]
            r_sqb = [r_sq512]
            sqs = [RB.alloc([256], F32) for _ in range(2)]
            r_sqs = [k.res("sqs%d" % i) for i in range(2)]
            tmps = [RB.alloc([256], F32) for _ in range(2)]
            r_tmps = [k.res("tmps%d" % i) for i in range(2)]
            PT = [RB.alloc([512], BF16) for _ in range(4)]
            r_PT = [k.res("PT%d" % i) for i in range(4)]
            tdiag = [RB.alloc([128], F32) for _ in range(2)]
            r_tdiag = [k.res("tdiag%d" % i) for i in range(2)]
            O0n = RB.alloc([NT, 128], F32)
            r_O0n = [k.res("O0n%d" % i) for i in range(NT)]
            finjunk = RB.alloc([128], F32)
            pss = RB.alloc([24], F32)
            r_pss = [k.res("pss%d" % i) for i in range(3)]
            fss = RB.alloc([16], F32)
            r_fss = [k.res("fss%d" % i) for i in range(16)]
            r_finjunk = k.res("finjunk")
            wfg = RC.alloc([8, 8], BF16)
            r_wfg = k.res("wfg", dma=True)

            wload(wkv, r_wkv, P["w_mem_kv"][0].rearrange("(c p) n -> p c n", p=128))
            wload(wfg, r_wfg, win_cols(O_FG, 8))
            k.op("pool", lambda e: e.memset(Vm.rearrange("p a (h d) -> p a h d", h=4)[:, :, :, 128:129], 1.0),
                 writes=[r_Vm])
            for i_ in range(2):
                k.op("pool", lambda e, i_=i_: e.memset(qka[i_][:, 2:4, 64:65], 1.0), writes=[r_qka[i_]])
            for mt in range(2):
                sl = mt % 2
                k.dma(AQ, xt[sl], mem_d[s, mt * 128:(mt + 1) * 128, :], writes=[r_xt[sl]], sem_res=r_xt[sl])
                rms_to_T(xt[sl], r_xt[sl], xn[sl], r_xn[sl], sqb[0], r_sqb[0], mn_c,
                         mhT[:, :, mt * 128:(mt + 1) * 128], r_mhT[mt], 7)
            for mt in range(2):
                for half in range(2):
                    pb = 5 + half
                    for c in range(8):
                        k.op("pe", lambda e, c=c, pb=pb, half=half, mt=mt: e.matmul(
                            psf(pb), mhT[:, c, mt * 128:(mt + 1) * 128], wkv[:, c, half * 512:(half + 1) * 512],
                            start=(c == 0), stop=(c == 7)),
                             reads=[r_mhT[mt], r_wkv], writes=[PS[pb]], signal=(c == 7), accum=True)
                    if half == 0:
                        qknorm(psf(pb), PS[pb], 4, 128, g_mk4, kbm, r_kbm, sq512, r_sq512, tmp512, r_tmp512)
                        for h in range(4):
                            k.op("pe", lambda e, h=h: e.transpose(psh(7)[:, h * 128:(h + 1) * 128], kbm[:, h, :], identb),
                                 reads=[r_kbm, r_idb], writes=[PS[7]], signal=(h == 3), accum=True)
                        k.op("act", lambda e, mt=mt: e.copy(kmT[:, :, mt * 128:(mt + 1) * 128],
                                                            psh(7)[:, 0:512].rearrange("p (h t) -> p h t", h=4)),
                             reads=[PS[7]], writes=[r_kmT])
                    else:
                        k.op("act", lambda e, mt=mt, pb=pb: e.copy(
                            Vm[:, mt, :].rearrange("p (h d) -> p h d", h=4)[:, :, 0:128],
                            psf(pb).rearrange("p (h d) -> p h d", h=4)), reads=[PS[pb]], writes=[r_Vm])

            for n_ in range(NT + 2):
                if n_ - 2 >= 0:
                    t = n_ - 2
                    rms_s3(xn[t % 2], r_xn[t % 2], nm_c, hT[:, :, t * 128:(t + 1) * 128], hT_res[t], 6 + t % 2)
                if 0 <= n_ - 1 < NT:
                    t = n_ - 1
                    rms_s2(t, xt[t % 2], r_xt[t % 2], xn[t % 2], r_xn[t % 2])
                if n_ < NT:
                    t = n_
                    k.dma(AQ, xt[t % 2], x_d[s, t * 128:(t + 1) * 128, :], writes=[r_xt[t % 2]], sem_res=r_xt[t % 2])
                    rms_s1(t, xt[t % 2], r_xt[t % 2], sqb[0], r_sqb[0])

            for t in range(NT):
                for c in range(8):
                    k.op("pe", lambda e, c=c, t=t: e.matmul(psf(6)[:, t * 8:(t + 1) * 8], hT[:, c, t * 128:(t + 1) * 128],
                                                           wfg[:, c, :], start=(c == 0), stop=(c == 7)),
                         reads=[hT_res[t], r_wfg], writes=[PS[6]], signal=(c == 7 and t == NT - 1), accum=True)
            k.op("dve", lambda e: e.tensor_tensor(zall, psf(6)[:, 0:NT * 8].rearrange("p (t h) -> p t h", t=NT),
                                                  fbias.unsqueeze(1).broadcast_to([128, NT, 8]), ALU.add),
                 reads=[PS[6], r_par], writes=[r_z])
            k.op("act", lambda e: e.activation(zall, zall, AF.Exp, scale=-1.0), reads=[r_z], writes=[r_z])
            k.op("act", lambda e: e.activation(zall, zall, AF.Ln, bias=1.0), reads=[r_z], writes=[r_z])
            k.op("dve", lambda e: e.tensor_scalar(logf, zall, -1.0, None, ALU.mult), reads=[r_z], writes=[r_logf])
            k.op("pe", lambda e: e.matmul(psf(5)[:, 0:NT * 8], ones_f, logf.rearrange("p t h -> p (t h)"),
                                          start=True, stop=True), reads=[r_logf, r_cst], writes=[PS[5]])
            k.op("dve", lambda e: e.memset(pref[:, 0, :], 0.0), writes=[r_pref])
            for t in range(1, NT):
                k.op("dve", lambda e, t=t: e.tensor_tensor(pref[:, t, :], pref[:, t - 1, :], psf(5)[:, (t - 1) * 8:t * 8], ALU.add),
                     reads=[PS[5], r_pref], writes=[r_pref])
            for t in range(NT):
                k.op("pe", lambda e, t=t: e.matmul(psf(6)[:, t * 8:(t + 1) * 8], Umat, logf[:, t, :], start=True, stop=True),
                     reads=[r_logf, r_cst], writes=[PS[6]], signal=(t == NT - 1), accum=True)
            k.op("dve", lambda e: e.tensor_tensor(pref, pref, psf(6)[:, 0:NT * 8].rearrange("p (t h) -> p t h", t=NT), ALU.add),
                 reads=[PS[6], r_pref], writes=[r_pref])
            k.op("dve", lambda e: e.tensor_scalar(negc, pref, -1.0, None, ALU.mult), reads=[r_pref], writes=[r_negc])
            k.op("dve", lambda e: e.tensor_scalar(c8, pref, 8.0, None, ALU.mult), reads=[r_pref], writes=[r_c8])

            k.barrier()

            pt_i = [0]
            st_i = [0]
            td_i = [0]

            def attention(qT_of, r_q, kT_of, r_k, nkt_of_g, causal, bias_of, Bmat, V_of, r_V, dvp1, scale, finish):
                for g in range(4):
                    nk = nkt_of_g(g)
                    pend = None
                    for j in range(nk + 1):
                        if j < nk:
                            col0 = 128 * (j - 4 * g) if (causal and j >= 4 * g) else 0
                            sb = 4 + (st_i[0] % 2)
                            st_i[0] += 1
                            q_r = [r_q[t] for t in range(4 * g + col0 // 128, 4 * g + 4)]
                            k.op("pe", lambda e, sb=sb, j=j, col0=col0, g=g: e.matmul(
                                psf(sb)[:, col0:512], kT_of(j), qT_of(512 * g + col0, 512 * (g + 1)),
                                start=True, stop=True), reads=q_r + [r_k[j]], writes=[PS[sb]])
                            pi = pt_i[0] % 4
                            pt_i[0] += 1
                            c1 = col0
                            if causal and j >= 4 * g:
                                di = td_i[0] % 2
                                td_i[0] += 1
                                k.op("dve", lambda e, sb=sb, col0=col0, di=di: e.scalar_tensor_tensor(
                                    tdiag[di], psf(sb)[:, col0:col0 + 128], scale, Bmat, ALU.mult, ALU.add),
                                     reads=[PS[sb], r_cst], writes=[r_tdiag[di]])
                                k.op("act", lambda e, di=di, pi=pi, col0=col0, j=j: e.activation(
                                    PT[pi][:, col0:col0 + 128], tdiag[di], AF.Exp, bias=bias_of(j), scale=1.0),
                                     reads=[r_tdiag[di], r_negc, r_cst], writes=[r_PT[pi]])
                                c1 = col0 + 128
                            if c1 < 512:
                                if bias_of is not None:
                                    k.op("act", lambda e, sb=sb, pi=pi, c1=c1, j=j: e.activation(
                                        PT[pi][:, c1:512], psf(sb)[:, c1:512], AF.Exp, bias=bias_of(j), scale=scale),
                                         reads=[PS[sb], r_negc, r_cst], writes=[r_PT[pi]])
                                else:
                                    k.op("act", lambda e, sb=sb, pi=pi, c1=c1: e.activation(
                                        PT[pi][:, c1:512], psf(sb)[:, c1:512], AF.Exp, scale=scale),
                                         reads=[PS[sb]], writes=[r_PT[pi]])
                            cur = (j, pi, col0)
                        else:
                            cur = None
                        if pend is not None:
                            jj, pi2, col02 = pend
                            had_fin = False
                            for i in range(col02 // 128, 4):
                                last = (jj == 4 * g + i) if causal else (jj == nk - 1)
                                k.op("pe", lambda e, i=i, pi2=pi2, jj=jj, last=last: e.matmul(
                                    psf(i)[:, 0:dvp1], PT[pi2][:, i * 128:(i + 1) * 128], V_of(jj),
                                    start=(jj == 0), stop=last),
                                     reads=[r_PT[pi2], r_V[jj]], writes=[PS[i]], accum=(jj != 0), signal=(i == 3))
                                if last:
                                    finish(4 * g + i, i)
                                    had_fin = True
                            if DUMMY_MM and not had_fin:
                                for b_ in range(4 - DUMMY_MM, 4):
                                    k.op("pe", lambda e, b_=b_: e.matmul(
                                        psf(b_)[:, 256:512], identb, PT[pi2][:, 0:256], start=False, stop=False,
                                        skip_group_check=True),
                                         reads=[r_PT[pi2], r_idb], writes=[PS[b_]], signal=False, accum=True)
                        pend = cur
                        tick()
                        yield

            units = [("da", h) for h in range(4)] + [("fx", p) for p in range(4)] + [("mem", h) for h in range(4)]
            units = units[:UNITS_LIMIT]

            def proj_gen(ui):
                kind, idx = units[ui]
                ub = ui % 2
                wsl = ui % 2
                if kind == "da":
                    offs = (O_AQ, O_AK, O_AV)
                elif kind == "fx":
                    offs = (O_FQ, O_FK, O_FV)
                else:
                    offs = (O_MQ,)
                for oi_, o_ in enumerate(offs):
                    wload(wu[wsl][:, :, oi_ * 128:(oi_ + 1) * 128], r_wu[wsl], win_cols(o_ + 128 * idx, 128))
                ncol = 128 * len(offs)
                if s == 0:
                    for p_ in range(2 * ui, min(2 * ui + 2, NFF)):
                        for part_ in range(2):
                            k.dma(WQ, wup_s[2 * p_ + part_].rearrange("p (c n) -> p c n", c=8),
                                  P["w_up"][0, :, part_ * DFF + 128 * p_:part_ * DFF + 128 * (p_ + 1)].rearrange(
                                      "(c p) n -> p c n", p=128), writes=[r_wups[p_]], sem_res=r_wups[p_])
                if kind == "da":
                    k.op("pool", lambda e: e.memset(Va[ub][:, :, 128:129], 1.0), writes=r_Va[ub])
                elif kind == "fx":
                    k.op("pool", lambda e: e.memset(
                        Va[ub].rearrange("p t (h d) -> p t h d", h=2)[:, :, :, 64:65], 1.0), writes=r_Va[ub])

                def tail(t):
                    qs = t % 2
                    if kind in ("da", "fx"):
                        for m in range(4):
                            k.op("pe", lambda e, m=m: e.transpose(psh(7)[0:65, m * 128:(m + 1) * 128],
                                                                   qka[qs][:, m, :], identb),
                                 reads=[r_qka[qs], r_idb], writes=[PS[7]], signal=(m == 3), accum=True)
                        k.op("dve", lambda e: e.tensor_copy(qkT[ub][0:65, :, t * 128:(t + 1) * 128],
                                                            psh(7)[0:65, 0:512].rearrange("p (m t) -> p m t", m=4)),
                             reads=[PS[7]], writes=[r_qT[ub][t]])
                    else:
                        k.op("pe", lambda e: e.transpose(psh(7)[:, 0:128], qmem[qs], identb),
                             reads=[r_qka[qs], r_idb], writes=[PS[7]])
                        k.op("act", lambda e: e.copy(qkT[ub][:, 0, t * 128:(t + 1) * 128], psh(7)[:, 0:128]),
                             reads=[PS[7]], writes=[r_qT[ub][t]])

                nqk = 256 if kind != "mem" else 128
                H, dd = (4, 64) if kind != "mem" else (1, 128)

                def stepA(t):
                    pb = 6
                    qs = t % 2
                    for c in range(8):
                        k.op("pe", lambda e, c=c: e.matmul(
                            psf(pb)[:, 0:ncol], hT[:, c, t * 128:(t + 1) * 128], wu[wsl][:, c, 0:ncol],
                            start=(c == 0), stop=(c == 7)),
                             reads=[hT_res[t], r_wu[wsl]], writes=[PS[pb]], signal=(c == 7), accum=True)
                    k.op("dve", lambda e: e.tensor_copy(praw[qs][:, 0:nqk], psf(pb)[:, 0:nqk]),
                         reads=[PS[pb]], writes=[r_praw[qs]])
                    if kind == "da":
                        k.op("act", lambda e: e.copy(Va[ub][:, t, 0:128], psf(pb)[:, 256:384]),
                             reads=[PS[pb]], writes=[r_Va[ub][t]])
                    elif kind == "fx":
                        k.op("act", lambda e: e.copy(
                            Va[ub][:, t, :].rearrange("p (h d) -> p h d", h=2)[:, :, 0:64],
                            psf(pb)[:, 256:384].rearrange("p (h d) -> p h d", h=2)),
                             reads=[PS[pb]], writes=[r_Va[ub][t]])
                    ss = pss[:, (t % 3) * 8:(t % 3) * 8 + 8]
                    r_ss = r_pss[t % 3]
                    k.op("act", lambda e: e.activation(sqs[qs][:, 0:nqk], praw[qs][:, 0:nqk], AF.Square),
                         reads=[r_praw[qs]], writes=[r_sqs[qs]])
                    k.op("dve", lambda e: e.tensor_reduce(ss[:, 0:H], sqs[qs][:, 0:nqk].rearrange("p (h d) -> p h d", h=H),
                                                          AX.X, ALU.add), reads=[r_sqs[qs]], writes=[r_ss])

                def stepB(t):
                    qs = t % 2
                    ss = pss[:, (t % 3) * 8:(t % 3) * 8 + 8]
                    r_ss = r_pss[t % 3]
                    rstd_from_ss(ss[:, 0:H], r_ss, H, 1.0 / dd)
                    k.op("dve", lambda e: e.tensor_tensor(
                        tmps[qs][:, 0:nqk].rearrange("p (h d) -> p h d", h=H),
                        praw[qs][:, 0:nqk].rearrange("p (h d) -> p h d", h=H),
                        ss[:, 0:H].unsqueeze(2).broadcast_to([128, H, dd]), ALU.mult),
                         reads=[r_praw[qs], r_ss], writes=[r_tmps[qs]])
                    if kind != "mem":
                        gt = g_da4 if kind == "da" else g_fx4
                        k.op("dve", lambda e: e.tensor_tensor(
                            qka[qs][:, :, 0:64], tmps[qs][:, 0:256].rearrange("p (h d) -> p h d", h=4),
                            gt.rearrange("p (h d) -> p h d", h=4), ALU.mult),
                             reads=[r_tmps[qs], r_par], writes=[r_qka[qs]])
                        if kind == "da":
                            k.op("dve", lambda e: e.tensor_copy(
                                qka[qs][:, 0:2, 64:65],
                                cst[:, C_DAQ + t * 4 + idx:C_DAQ + t * 4 + idx + 1].unsqueeze(1).broadcast_to([128, 2, 1])),
                                 reads=[r_cst], writes=[r_qka[qs]])
                        else:
                            k.op("dve", lambda e: e.tensor_copy(
                                qka[qs][:, 0:2, 64:65], c8[:, t, 2 * idx:2 * idx + 2].unsqueeze(2)),
                                 reads=[r_c8], writes=[r_qka[qs]])
                    else:
                        k.op("dve", lambda e: e.tensor_tensor(
                            qmem[qs], tmps[qs][:, 0:128], g_mq, ALU.mult),
                             reads=[r_tmps[qs], r_par], writes=[r_qka[qs]])

                for n_ in range(NT + 2):
                    if n_ - 2 >= 0:
                        tail(n_ - 2)
                    if 0 <= n_ - 1 < NT:
                        stepB(n_ - 1)
                    if n_ < NT:
                        stepA(n_)
                    yield

            fss_i = [0]
            deferred = []

            def defer(n, fn):
                deferred.append([n, fn])

            def tick(flush=False):
                for d_ in list(deferred):
                    d_[0] -= 1
                    if d_[0] <= 0 or flush:
                        deferred.remove(d_)
                        d_[1]()

            def attn_gen(ui):
                kind, idx = units[ui]
                ub = ui % 2
                if kind == "da":
                    h = idx
                    for m in range(2):
                        def fin(tile, i, m=m, h=h):
                            rc, r_rc = get_small()
                            k.op("dve", lambda e: e.reciprocal(rc[:, 0:1], psf(i)[:, 128:129]),
                                 reads=[PS[i]], writes=[r_rc])
                            if m == 0:
                                k.op("dve", lambda e: e.tensor_scalar(O0n[:, tile, :], psf(i)[:, 0:128], rc[:, 0:1], None, ALU.mult),
                                     reads=[PS[i], r_rc], writes=[r_O0n[tile]])
                            else:
                                k.op("dve", lambda e: e.tensor_tensor(rc[:, 1:2], rc[:, 0:1], neg_lam, ALU.mult),
                                     reads=[r_rc, r_lam], writes=[r_rc])
                                k.op("dve", lambda e: e.scalar_tensor_tensor(
                                    O0n[:, tile, :], psf(i)[:, 0:128], rc[:, 1:2], O0n[:, tile, :], ALU.mult, ALU.add),
                                     reads=[PS[i], r_rc], writes=[r_O0n[tile]])
                                fi = fss_i[0] % 16
                                fss_i[0] += 1

                                def stB1():
                                    k.op("act", lambda e: e.activation(finjunk, O0n[:, tile, :], AF.Square, accum_out=fss[:, fi:fi + 1]),
                                         reads=[r_O0n[tile]], writes=[r_finjunk, r_fss[fi]])
                                    rstd_from_ss(fss[:, fi:fi + 1], r_fss[fi], 1, 1.0 / 128)

                                def stB2():
                                    k.op("dve", lambda e: e.scalar_tensor_tensor(
                                        o_br[0][:, tile, h * 128:(h + 1) * 128], O0n[:, tile, :], fss[:, fi:fi + 1], g_sub,
                                        ALU.mult, ALU.mult),
                                         reads=[r_O0n[tile], r_fss[fi], r_par], writes=[r_obr[0][tile]])
                                defer(2, stB1)
                                defer(4, stB2)
                        yield from attention(lambda c0, c1, m=m: qkT[ub][0:65, m, c0:c1], r_qT[ub],
                                             lambda j, m=m: qkT[ub][0:65, 2 + m, j * 128:(j + 1) * 128], r_qT[ub],
                                             lambda g: 4 * g + 4, True,
                                             lambda j, h=h: cst[:, C_ALK + j * 4 + h:C_ALK + j * 4 + h + 1],
                                             cst[:, C_BDA + 128 * h:C_BDA + 128 * (h + 1)],
                                             lambda j: Va[ub][:, j, 0:129], r_Va[ub], 129, 0.125, fin)
                    tick(flush=True)
                elif kind == "fx":
                    for m in range(2):
                        hd = 2 * idx + m

                        def fin(tile, i, hd=hd):
                            rc, r_rc = get_small()
                            k.op("dve", lambda e: e.reciprocal(rc[:, 0:1], psf(i)[:, 64:65]), reads=[PS[i]], writes=[r_rc])
                            k.op("dve", lambda e: e.tensor_scalar(o_br[1][:, tile, hd * 64:(hd + 1) * 64], psf(i)[:, 0:64],
                                                                  rc[:, 0:1], None, ALU.mult),
                                 reads=[PS[i], r_rc], writes=[r_obr[1][tile]])
                        yield from attention(lambda c0, c1, m=m: qkT[ub][0:65, m, c0:c1], r_qT[ub],
                                             lambda j, m=m: qkT[ub][0:65, 2 + m, j * 128:(j + 1) * 128], r_qT[ub],
                                             lambda g: 4 * g + 4, True,
                                             lambda j, hd=hd: negc[:, j, hd:hd + 1],
                                             cst[:, C_BFX:C_BFX + 128],
                                             lambda j, m=m: Va[ub][:, j, m * 65:(m + 1) * 65], r_Va[ub], 65, 0.125, fin)
                else:
                    h = idx

                    def fin(tile, i, h=h):
                        rc, r_rc = get_small()
                        k.op("dve", lambda e: e.reciprocal(rc[:, 0:1], psf(i)[:, 128:129]), reads=[PS[i]], writes=[r_rc])
                        k.op("dve", lambda e: e.tensor_scalar(o_br[2][:, tile, h * 128:(h + 1) * 128], psf(i)[:, 0:128],
                                                              rc[:, 0:1], None, ALU.mult),
                             reads=[PS[i], r_rc], writes=[r_obr[2][tile]])
                    yield from attention(lambda c0, c1: qkT[ub][:, 0, c0:c1], r_qT[ub],
                                         lambda j, h=h: kmT[:, h, j * 128:(j + 1) * 128], [r_kmT, r_kmT],
                                         lambda g: 2, False, None, None,
                                         lambda j, h=h: Vm[:, j, h * 129:(h + 1) * 129], [r_Vm, r_Vm], 129, 128 ** -0.5, fin)

            def run(gen):
                for _ in gen:
                    pass

            prev = None
            for ui in range(len(units)):
                pg = proj_gen(ui)
                if prev is None or not OPT_INTERLEAVE:
                    if prev is not None:
                        run(prev)
                    run(pg)
                else:
                    na = 88 if units[ui - 1][0] != "mem" else 12
                    for i_ in range(NT):
                        for _ in range(na * (i_ + 1) // NT - na * i_ // NT):
                            next(prev, None)
                        next(pg, None)
                    run(prev)
                    run(pg)
                prev = attn_gen(ui) if not SKIP_ATTN else None
            if prev is not None:
                run(prev)

            if dbg and s == 0:
                dump(o_br[0][:, 0, :], r_obr[0], 512)
                dump(o_br[1][:, 0, :], r_obr[1], 512)
                dump(o_br[2][:, 0, :], r_obr[2], 512)
                dump(o_br[0][:, 15, :], r_obr[0], 512)
                dump(o_br[1][:, 15, :], r_obr[1], 512)
                dump(o_br[2][:, 15, :], r_obr[2], 512)
            k.barrier()

            if STOP_AFTER <= 1:
                continue
            RB.reset(); RC.reset()
            merged = RB.alloc([NT, D], BF16)
            r_mg = [k.res("mg%d" % t) for t in range(NT)]
            RA2 = Bump(arena, RA0 + 48 * 1024, RB0)
            Wg = RC.alloc([8, 3 * 512], BF16)
            r_Wg = [k.res("Wg%d" % i, dma=True) for i in range(3)]
            wb = RC.alloc([4, 3 * 512], BF16)
            r_wb = [k.res("wb%d" % i, dma=True) for i in range(3)]
            bg = RC.alloc([3 * D], F32)
            r_bg = k.res("bg", dma=True)
            oT = [RA2.alloc([12, 128], BF16) for _ in range(2)]
            r_oT = [k.res("oT%d" % i) for i in range(2)]
            gtmp = [RC.alloc([512], F32) for _ in range(2)]
            r_gtmp = [k.res("gtmp%d" % i) for i in range(2)]
            pr = [RC.alloc([512], F32) for _ in range(3)]
            r_pr = [k.res("pr%d" % i) for i in range(3)]
            for i in range(3):
                k.dma(AQ, bg[:, i * D:(i + 1) * D], P["b_gate"][0, i:i + 1, :].partition_broadcast(128),
                      writes=[r_bg], sem_res=r_bg)
            wbn = ["w_branch_da", "w_branch_fx", "w_branch_mem"]
            gi = [0]
            for n in range(2):
                for i in range(3):
                    wload(wb[:, :, i * 512:(i + 1) * 512], r_wb[i],
                          P[wbn[i]][0, :, n * 512:(n + 1) * 512].rearrange("(c p) n -> p c n", p=128))
                    wload(Wg[:, :, i * 512:(i + 1) * 512], r_Wg[i], win_cols(O_G + i * D + n * 512, 512))
                def emit_T(t):
                    osl = t % 2
                    for rnd in range(2):
                        pbk = rnd
                        lo, hi = (0, 8) if rnd == 0 else (8, 12)
                        for cc in range(lo, hi):
                            br, c4 = cc // 4, cc % 4
                            k.op("pe", lambda e, cc=cc, br=br, c4=c4: e.transpose(
                                psh(pbk)[:, (cc - lo) * 128:(cc - lo + 1) * 128],
                                o_br[br][:, t, c4 * 128:(c4 + 1) * 128], identb),
                                 reads=[r_obr[br][t], r_idb], writes=[PS[pbk]], signal=(cc == hi - 1), accum=True)
                        k.op("act", lambda e: e.copy(
                            oT[osl][:, lo:hi, :], psh(pbk)[:, 0:(hi - lo) * 128].rearrange("p (c t) -> p c t", c=hi - lo)),
                             reads=[PS[pbk]], writes=[r_oT[osl]])

                emit_T(0)
                for t in range(NT):
                    osl = t % 2
                    if t + 1 < NT:
                        emit_T(t + 1)
                    for i in range(3):
                        yb, gb = 2 + i, 5 + i
                        for c in range(4):
                            k.op("pe", lambda e, c=c, i=i, yb=yb, osl=osl: e.matmul(
                                psf(yb), oT[osl][:, 4 * i + c, :], wb[:, c, i * 512:(i + 1) * 512],
                                start=(c == 0), stop=(c == 3)),
                                 reads=[r_oT[osl], r_wb[i]], writes=[PS[yb]], signal=(c == 3), accum=True)
                        for c in range(8):
                            k.op("pe", lambda e, c=c, i=i, gb=gb, t=t: e.matmul(
                                psf(gb), hT[:, c, t * 128:(t + 1) * 128], Wg[:, c, i * 512:(i + 1) * 512],
                                start=(c == 0), stop=(c == 7)),
                                 reads=[hT_res[t], r_Wg[i]], writes=[PS[gb]], signal=(c == 7), accum=True)
                        gs = gi[0] % 2
                        gi[0] += 1
                        k.op("dve", lambda e, gs=gs, gb=gb, i=i, n=n: e.tensor_tensor(
                            gtmp[gs], psf(gb), bg[:, i * D + n * 512:i * D + (n + 1) * 512], ALU.add),
                             reads=[PS[gb], r_bg], writes=[r_gtmp[gs]])
                        k.op("act", lambda e, gs=gs: e.activation(gtmp[gs], gtmp[gs], AF.Sigmoid),
                             reads=[r_gtmp[gs]], writes=[r_gtmp[gs]])
                        k.op("dve", lambda e, gs=gs, yb=yb, i=i: e.tensor_tensor(pr[i], gtmp[gs], psf(yb), ALU.mult),
                             reads=[r_gtmp[gs], PS[yb]], writes=[r_pr[i]])
                    k.op("pool", lambda e: e.tensor_tensor(pr[0], pr[0], pr[1], ALU.add),
                         reads=[r_pr[0], r_pr[1]], writes=[r_pr[0]])
                    k.op("pool", lambda e, t=t, n=n: e.tensor_tensor(merged[:, t, n * 512:(n + 1) * 512], pr[0], pr[2], ALU.add),
                         reads=[r_pr[0], r_pr[2]], writes=[r_mg[t]])
            if dbg and s == 0:
                dump(merged[:, 0, :], r_mg, 1024)
            k.barrier()

            RA.reset(); RC.reset()
            x1 = RA.alloc([NT, D], F32)
            r_x1 = [k.res("x1_%d" % t, dma=True) for t in range(NT)]
            wo = RC.alloc([8, D], BF16)
            r_wo = k.res("wo", dma=True)
            xt = [RC.alloc([D], F32) for _ in range(2)]
            r_xt = [k.res("xt3_%d" % i, dma=True) for i in range(2)]
            xn = [RC.alloc([D], BF16) for _ in range(2)]
            r_xn = [k.res("xn3_%d" % i) for i in range(2)]
            sqb3 = RC.alloc([D], F32)
            r_sqb3 = k.res("sqb3")
            mT = [RC.alloc([8, 128], BF16) for _ in range(2)]
            r_mT = [k.res("mT%d" % i) for i in range(2)]
            wload(wo, r_wo, P["w_out"][0].rearrange("(c p) n -> p c n", p=128))
            for n_ in range(NT + 3):
                if n_ - 3 >= 0:
                    t = n_ - 3
                    rms_s3(xn[t % 2], r_xn[t % 2], nf_c, hT[:, :, t * 128:(t + 1) * 128], hT_res[t], 6 + t % 2)
                if 0 <= n_ - 2 < NT:
                    t = n_ - 2
                    rms_s2(t, x1[:, t, :], r_x1[t], xn[t % 2], r_xn[t % 2])
                if 0 <= n_ - 1 < NT:
                    t = n_ - 1
                    sl = t % 2
                    for n in range(2):
                        ob = 2 + 2 * sl + n
                        for c in range(8):
                            k.op("pe", lambda e, c=c: e.matmul(
                                psf(ob), mT[sl][:, c, :], wo[:, c, n * 512:(n + 1) * 512], start=(c == 0), stop=(c == 7)),
                                 reads=[r_mT[sl], r_wo], writes=[PS[ob]], signal=(c == 7), accum=True)
                        k.op("dve", lambda e: e.tensor_tensor(
                            x1[:, t, n * 512:(n + 1) * 512], psf(ob), xt[sl][:, n * 512:(n + 1) * 512], ALU.add),
                             reads=[PS[ob], r_xt[sl]], writes=[r_x1[t]])
                    rms_s1(t, x1[:, t, :], r_x1[t], sqb3, r_sqb3)
                if n_ < NT:
                    t = n_
                    sl = t % 2
                    k.dma(AQ, xt[sl], x_d[s, t * 128:(t + 1) * 128, :], writes=[r_xt[sl]], sem_res=r_xt[sl])
                    for c in range(8):
                        k.op("pe", lambda e, c=c: e.transpose(
                            psh(sl)[:, c * 128:(c + 1) * 128], merged[:, t, c * 128:(c + 1) * 128], identb),
                             reads=[r_mg[t], r_idb], writes=[PS[sl]], signal=(c == 7), accum=True)
                    k.op("act", lambda e: e.copy(mT[sl], psh(sl).rearrange("p (c t) -> p c t", c=8)),
                         reads=[PS[sl]], writes=[r_mT[sl]])
            if dbg and s == 0:
                dump(x1[:, 0, :], r_x1, 1024)
            k.barrier()

            if STOP_AFTER <= 3:
                continue
            RB.reset(); RC.reset()
            actT = RB.alloc([NFF, 512], BF16)
            r_actT = [k.res("actT%d" % p) for p in range(NFF)]
            ubuf = [RB.alloc([2, 514], F32) for _ in range(2)]
            r_ub = [[k.res("ub%d_%d" % (i, a)) for a in range(2)] for i in range(2)]
            wd = RC.alloc([NFF, D], BF16)
            r_wd = k.res("wd", dma=True)
            wup = [RC.alloc([2, 1024], BF16) for _ in range(2)]
            r_wup = [k.res("wup%d" % i, dma=True) for i in range(2)]
            uc = [[RC.alloc([512], F32) for _ in range(2)] for _ in range(2)]
            r_uc = [[k.res("uc%d_%d" % (i, a)) for a in range(2)] for i in range(2)]
            carry = RC.alloc([2 * NFF, 2], F32)
            r_carry = [k.res("carry%d" % i) for i in range(2 * NFF)]
            wload(wd, r_wd, P["w_down"][0].rearrange("(c p) n -> p c n", p=128))
            k.op("pool", lambda e: e.memset(carry, 0.0), writes=r_carry)
            cb_c, cw2_c, cw0_c, cw1_c = 24, 68, 112, 156
            wi = [0]
            pbi = [0]
            cti = [0]
            for tg in range(4):
                for p in range(NFF):
                    ws = wi[0] % 2
                    wi[0] += 1
                    us = p % 2
                    k.dma(AQ, wup[ws], wup_s[2 * p:2 * p + 2].rearrange("b p n -> p b n"),
                          reads=[r_wups[p]], writes=[r_wup[ws]], sem_res=r_wup[ws])
                    for part in range(2):
                        ch = p + NFF * part
                        pb = pbi[0] % 4
                        pbi[0] += 1
                        for c in range(8):
                            k.op("pe", lambda e, c=c: e.matmul(
                                psf(pb), wup[ws][:, part, c * 128:(c + 1) * 128], hT[:, c, tg * 512:(tg + 1) * 512],
                                start=(c == 0), stop=(c == 7)),
                                 reads=[r_wup[ws]] + hT_res[4 * tg:4 * tg + 4], writes=[PS[pb]], signal=(c == 7), accum=True)
                        ub_ = ubuf[us]
                        k.op("act", lambda e: e.copy(ub_[:, part, 0:2], carry[:, ch, :]),
                             reads=[r_carry[ch]], writes=[r_ub[us][part]])
                        k.op("act", lambda e: e.copy(ub_[:, part, 2:514], psf(pb)),
                             reads=[PS[pb]], writes=[r_ub[us][part]])
                        k.op("act", lambda e: e.copy(carry[:, ch, :], ub_[:, part, 512:514]),
                             reads=[r_ub[us][part]], writes=[r_carry[ch]])
                        k.op("act", lambda e: e.activation(uc[us][part], psf(pb), AF.Identity,
                                                           bias=pcol[:, cb_c + ch:cb_c + ch + 1],
                                                           scale=pcol[:, cw2_c + ch:cw2_c + ch + 1]),
                             reads=[PS[pb], r_pcol], writes=[r_uc[us][part]])
                        k.op("dve", lambda e: e.scalar_tensor_tensor(
                            uc[us][part], ub_[:, part, 1:513], pcol[:, cw1_c + ch:cw1_c + ch + 1], uc[us][part], ALU.mult, ALU.add),
                             reads=[r_ub[us][part], r_pcol], writes=[r_uc[us][part]])
                        k.op("dve", lambda e: e.scalar_tensor_tensor(
                            uc[us][part], ub_[:, part, 0:512], pcol[:, cw0_c + ch:cw0_c + ch + 1], uc[us][part], ALU.mult, ALU.add),
                             reads=[r_ub[us][part], r_pcol], writes=[r_uc[us][part]])
                    k.op("act", lambda e: e.activation(uc[us][0], uc[us][0], AF.Silu),
                         reads=[r_uc[us][0]], writes=[r_uc[us][0]])
                    k.op("dve", lambda e: e.tensor_tensor(actT[:, p, :], uc[us][0], uc[us][1], ALU.mult),
                         reads=[r_uc[us][0], r_uc[us][1]], writes=[r_actT[p]])
                for tt in range(4):
                    t = tg * 4 + tt
                    osl = t % 2
                    for n in range(2):
                        ob = 4 + (2 * t + n) % 4
                        for p in range(NFF):
                            k.op("pe", lambda e, p=p, n=n, ob=ob, tt=tt: e.matmul(
                                psf(ob), actT[:, p, tt * 128:(tt + 1) * 128], wd[:, p, n * 512:(n + 1) * 512],
                                start=(p == 0), stop=(p == NFF - 1)),
                                 reads=[r_actT[p], r_wd], writes=[PS[ob]], signal=(p == NFF - 1), accum=True)
                        k.op("dve", lambda e, n=n, ob=ob, t=t: e.tensor_tensor(
                            x1[:, t, n * 512:(n + 1) * 512], psf(ob), x1[:, t, n * 512:(n + 1) * 512], ALU.add),
                             reads=[PS[ob]], writes=[r_x1[t]])
                    k.dma(AQ, out_d[s, t * 128:(t + 1) * 128, :], x1[:, t, :], reads=[r_x1[t]], sem_res=r_x1[t])
            k.barrier()
        k.final()
    return nc


_NC_CACHE = {}


def kernel(**inputs):
    n = 8
    x = np.ascontiguousarray(inputs["x"], dtype=np.float32)
    mem = np.ascontiguousarray(inputs["mem"], dtype=np.float32)
    B = x.shape[0]
    per = B // n
    if "nc" not in _NC_CACHE:
        _NC_CACHE["nc"] = build(per)
    nc = _NC_CACHE["nc"]
    cst = make_consts()
    params = {nme: np.ascontiguousarray(inputs[nme], dtype=np.float32) for nme in PARAM_NAMES}
    in_maps = []
    for i in range(n):
        m = {"x": x[i * per:(i + 1) * per], "mem": mem[i * per:(i + 1) * per], "cst": cst}
        m.update(params)
        in_maps.append(m)
    res = run_bass_kernel_spmd(nc, in_maps, core_ids=list(range(n)))
    return np.concatenate([r["out"] for r in res.results], axis=0)
```
